# Optimizing a Trainium2 kernel written in Bass

```python
import math
import jax, jax.numpy as jnp
from jax import lax
import numpy as np

D_MODEL = 1024
BATCH = 2
SEQ = 8192
DEPTH = 1

MIX_WIDTH = D_MODEL
POOL_WIDTH = MIX_WIDTH // 2
POOL_WINDOWS = (2, 4, 8, 16)
POOL_GROUPS = len(POOL_WINDOWS)
POOL_CH = POOL_WIDTH // POOL_GROUPS
ATTN_WIDTH = MIX_WIDTH - POOL_WIDTH
HEAD_DIM = 64
ATTN_HEADS = ATTN_WIDTH // HEAD_DIM
IN_COLS = POOL_WIDTH + 3 * ATTN_WIDTH
BLOCK = 256
TOP_K_BLOCKS = 3
QCHUNK = 64
NUM_BUCKETS = 32
MAX_DISTANCE = 1024
D_FF = 4 * D_MODEL
PLE_DIM = 256
EPS = 1e-6
NEG = -1e30

kernel_name = "hybrid_pool_moba_layer"


def rmsnorm(x, g):
    xf = x.astype(jnp.float32)
    y = xf * lax.rsqrt(jnp.mean(xf * xf, axis=-1, keepdims=True) + EPS)
    return (y * g.astype(jnp.float32)).astype(x.dtype)


def t5_bucket(rel):
    n = jnp.maximum(rel, 0)
    max_exact = NUM_BUCKETS // 2
    nf = jnp.maximum(n, 1).astype(jnp.float32)
    large = max_exact + (jnp.log(nf / max_exact) / math.log(MAX_DISTANCE / max_exact)
                         * (NUM_BUCKETS - max_exact)).astype(jnp.int32)
    large = jnp.minimum(large, NUM_BUCKETS - 1)
    return jnp.where(n < max_exact, n, large)


def pool_mixer(u, w_pool, scale):
    B, S, _ = u.shape
    ug = u.astype(jnp.float32).reshape(B, S, POOL_GROUPS, POOL_CH)
    c = jnp.pad(jnp.cumsum(ug, axis=1), ((0, 0), (1, 0), (0, 0), (0, 0)))
    t = jnp.arange(S)
    outs = []
    for g, w in enumerate(POOL_WINDOWS):
        cg = c[:, :, g]
        lagged = jnp.pad(cg, ((0, 0), (w, 0), (0, 0)))[:, 1:S + 1]
        cnt = jnp.minimum(t + 1, w).astype(jnp.float32)[None, :, None]
        outs.append((cg[:, 1:] - lagged) / cnt - ug[:, :, g])
    d = jnp.stack(outs, axis=2)
    y = jnp.einsum('bsgc,gcd->bsgd', d, w_pool.astype(jnp.float32))
    return (y.reshape(B, S, POOL_WIDTH) * scale.astype(jnp.float32)).astype(u.dtype)


def moba_attention(q, k, v, rel_bias):
    B, S, H, Dh = q.shape
    nb = -(-S // BLOCK)
    sp = nb * BLOCK
    pad = ((0, 0), (0, sp - S), (0, 0), (0, 0))
    q, k, v = [jnp.pad(a, pad).transpose(0, 2, 1, 3) for a in (q, k, v)]
    kb = k.reshape(B, H, nb, BLOCK, Dh)
    vb = v.reshape(B, H, nb, BLOCK, Dh)
    kbar = jnp.mean(kb.astype(jnp.float32), axis=3)
    pos = jnp.arange(sp)
    qblk = pos // BLOCK
    ksel = min(TOP_K_BLOCKS, nb)
    bscore = jnp.einsum('bhsd,bhnd->bhsn', q.astype(jnp.float32), kbar)
    past = jnp.arange(nb)[None, :] < qblk[:, None]
    bscore = jnp.where(past[None, None], bscore, NEG)
    _, idx = lax.top_k(bscore, ksel)
    valid = jnp.arange(ksel)[None, :] < qblk[:, None]
    tab = rel_bias.T.astype(jnp.float32)
    scale = HEAD_DIM ** -0.5
    bi = jnp.arange(B)[:, None, None, None]
    hi = jnp.arange(H)[None, :, None, None]
    hi5 = jnp.arange(H)[None, :, None, None, None]

    def chunk(ci):
        s0 = ci * QCHUNK
        qc = lax.dynamic_slice_in_dim(q, s0, QCHUNK, axis=2)
        ic = lax.dynamic_slice_in_dim(idx, s0, QCHUNK, axis=2)
        vc = lax.dynamic_slice_in_dim(valid, s0, QCHUNK, axis=0)
        qpos = s0 + jnp.arange(QCHUNK)
        kg = kb[bi, hi, ic]
        vg = vb[bi, hi, ic]
        kpos_sel = ic[..., None] * BLOCK + jnp.arange(BLOCK)
        bias_sel = tab[hi5, t5_bucket(qpos[None, None, :, None, None] - kpos_sel)]
        l_sel = jnp.einsum('bhqd,bhqjkd->bhqjk', qc, kg).astype(jnp.float32) * scale + bias_sel
        l_sel = jnp.where(vc[None, None, :, :, None], l_sel, NEG)
        ob0 = (s0 // BLOCK) * BLOCK
        ko = lax.dynamic_slice_in_dim(k, ob0, BLOCK, axis=2)
        vo = lax.dynamic_slice_in_dim(v, ob0, BLOCK, axis=2)
        rel_own = qpos[:, None] - (ob0 + jnp.arange(BLOCK))[None, :]
        bias_own = tab[:, t5_bucket(rel_own)]
        l_own = jnp.einsum('bhqd,bhkd->bhqk', qc, ko).astype(jnp.float32) * scale + bias_own[None]
        l_own = jnp.where((rel_own >= 0)[None, None], l_own, NEG)
        logits = jnp.concatenate([l_sel.reshape(B, H, QCHUNK, ksel * BLOCK), l_own], axis=-1)
        pr = jax.nn.softmax(logits, axis=-1).astype(v.dtype)
        p_sel = pr[..., :ksel * BLOCK].reshape(B, H, QCHUNK, ksel, BLOCK)
        p_own = pr[..., ksel * BLOCK:]
        return (jnp.einsum('bhqjk,bhqjkd->bhqd', p_sel, vg)
                + jnp.einsum('bhqk,bhkd->bhqd', p_own, vo))

    outs = lax.map(chunk, jnp.arange(sp // QCHUNK))
    out = outs.transpose(1, 2, 0, 3, 4).reshape(B, H, sp, Dh)[:, :, :S]
    return out.transpose(0, 2, 1, 3)


def setup_inputs(seed: int = 0) -> dict:
    key = jax.random.key(seed)
    ks = jax.random.split(key, 16)
    f32 = jnp.float32
    nrm = lambda k, shape, fan: jax.random.normal(k, shape, f32) * (fan ** -0.5)
    gain = lambda k: 1.0 + 0.02 * jax.random.normal(k, (DEPTH, D_MODEL), f32)
    return {
        "x": jax.random.normal(ks[0], (BATCH, SEQ, D_MODEL), f32),
        "p": jax.random.normal(ks[1], (DEPTH, BATCH, SEQ, PLE_DIM), f32),
        "w_in": nrm(ks[2], (DEPTH, D_MODEL, IN_COLS), D_MODEL),
        "w_pool": nrm(ks[3], (DEPTH, POOL_GROUPS, POOL_CH, POOL_CH), POOL_CH),
        "pool_scale": 1.0 + 0.02 * jax.random.normal(ks[4], (DEPTH, POOL_WIDTH), f32),
        "w_out": nrm(ks[5], (DEPTH, MIX_WIDTH, D_MODEL), MIX_WIDTH),
        "rel_bias": 0.5 * jax.random.normal(ks[6], (NUM_BUCKETS, ATTN_HEADS), f32),
        "g_mix_pre": gain(ks[7]),
        "g_mix_post": gain(ks[8]),
        "g_mlp_pre": gain(ks[9]),
        "g_mlp_post": gain(ks[10]),
        "w_up": nrm(ks[11], (DEPTH, D_MODEL, D_FF), D_MODEL),
        "w_down": nrm(ks[12], (DEPTH, D_FF, D_MODEL), D_FF),
        "w_ple_proj": nrm(ks[13], (DEPTH, PLE_DIM, D_MODEL), PLE_DIM),
        "w_ple_gate": nrm(ks[14], (DEPTH, D_MODEL, D_MODEL), D_MODEL),
    }


def reference(x, p, w_in, w_pool, pool_scale, w_out, rel_bias, g_mix_pre, g_mix_post,
              g_mlp_pre, g_mlp_post, w_up, w_down, w_ple_proj, w_ple_gate):
    B, S, _ = x.shape
    h = x
    for i in range(DEPTH):
        a = rmsnorm(h, g_mix_pre[i])
        z = a @ w_in[i]
        z_pool = z[..., :POOL_WIDTH]
        q = z[..., POOL_WIDTH:POOL_WIDTH + ATTN_WIDTH].reshape(B, S, ATTN_HEADS, HEAD_DIM)
        k = z[..., POOL_WIDTH + ATTN_WIDTH:POOL_WIDTH + 2 * ATTN_WIDTH].reshape(B, S, ATTN_HEADS, HEAD_DIM)
        v = z[..., POOL_WIDTH + 2 * ATTN_WIDTH:].reshape(B, S, ATTN_HEADS, HEAD_DIM)
        y_pool = pool_mixer(z_pool, w_pool[i], pool_scale[i])
        y_attn = moba_attention(q, k, v, rel_bias).reshape(B, S, ATTN_WIDTH)
        mix = jnp.concatenate([y_pool, y_attn.astype(y_pool.dtype)], axis=-1) @ w_out[i]
        h = h + rmsnorm(mix, g_mix_post[i])
        m = rmsnorm(h, g_mlp_pre[i])
        f = jnp.square(jax.nn.relu(m @ w_up[i])) @ w_down[i]
        h = h + rmsnorm(f, g_mlp_post[i])
        h = h + jax.nn.sigmoid(h @ w_ple_gate[i]) * (p[i] @ w_ple_proj[i])
    return h
```

```python
import math
from contextlib import ExitStack

import numpy as np
import concourse.bass as bass
import concourse.mybir as mybir
from concourse.bass_utils import run_bass_kernel_spmd

F32 = mybir.dt.float32
BF16 = mybir.dt.bfloat16
AF = mybir.ActivationFunctionType
ALU = mybir.AluOpType

NCORES = 8
S = 8192
D = 1024
NB = 32
NOWN = 2048
EPS = 1e-6
MASKV = 30000.0
ENGS = ("pe", "act", "dve", "pool", "sp")
SB_BASE = 17408
VS = 80


class Op:
    __slots__ = ("eng", "fn", "deps", "dma", "dkey", "dval", "sigval", "idx", "waits")

    def __init__(self, eng, fn, dma, dkey):
        self.eng = eng
        self.fn = fn
        self.deps = set()
        self.dma = dma
        self.dkey = dkey
        self.dval = 0
        self.sigval = None
        self.waits = []


class _Rec:
    def __init__(self):
        self.call = None

    def __getattr__(self, name):
        def f(*a, **k):
            self.call = (name, a, k)
            return None
        return f


class Prog:
    def __init__(self):
        self.ops = []
        self.per_eng = {e: [] for e in ENGS}
        self.last_w = {}
        self.readers = {}
        self.dma_count = {}

    def _add(self, eng, fn, reads, writes, dma, dkey):
        rec = _Rec()
        fn(rec)
        assert rec.call is not None
        op = Op(eng, rec.call, dma, dkey)
        op.idx = len(self.ops)
        for r in reads:
            w = self.last_w.get(r)
            if w is not None:
                op.deps.add(w)
        for k in writes:
            w = self.last_w.get(k)
            if w is not None:
                op.deps.add(w)
            for rd in self.readers.get(k, ()):
                op.deps.add(rd)
        for k in writes:
            self.last_w[k] = op.idx
            self.readers[k] = []
        for r in reads:
            self.readers.setdefault(r, []).append(op.idx)
        op.deps.discard(op.idx)
        if dma:
            c = self.dma_count.get(dkey, 0) + 16
            self.dma_count[dkey] = c
            op.dval = c
        self.ops.append(op)
        self.per_eng[eng].append(op)
        return op

    def op(self, eng, fn, reads=(), writes=()):
        return self._add(eng, fn, tuple(reads), tuple(writes), False, None)

    def dma(self, eng, fn, reads=(), writes=(), key=None):
        writes = tuple(writes)
        if key is None:
            key = writes[0]
        if eng == "pool":
            self.npool = getattr(self, "npool", 0) + 1
            writes = writes + (("poolq", self.npool % 2),)
            key = ("pq", self.npool % 2)
        return self._add(eng, fn, tuple(reads), writes, True, key)

    def barrier(self, new_keys=()):
        keys = list(set(self.last_w) | set(self.readers))
        self._add("sp", lambda e: e.nop(), tuple(keys), tuple(keys) + tuple(new_keys), False, None)

    def finalize(self):
        need = set()
        for op in self.ops:
            for d in op.deps:
                dop = self.ops[d]
                if dop.dma:
                    continue
                if dop.eng == "pe" and op.eng == "pe" and not op.dma:
                    continue
                need.add(d)
        cnt = {e: 0 for e in ENGS}
        for op in self.ops:
            if op.dma:
                continue
            if op.idx in need:
                cnt[op.eng] += 1
                op.sigval = cnt[op.eng]
        seen = {e: {} for e in ENGS}
        for e in ENGS:
            for op in self.per_eng[e]:
                req = {}
                for d in op.deps:
                    dop = self.ops[d]
                    if dop.dma:
                        k = ("d", dop.dkey)
                        v = dop.dval
                    else:
                        if dop.eng == "pe" and op.eng == "pe" and not op.dma:
                            continue
                        k = ("e", dop.eng)
                        v = dop.sigval
                    if v > req.get(k, 0):
                        req[k] = v
                for k, v in req.items():
                    if v > seen[e].get(k, 0):
                        seen[e][k] = v
                        op.waits.append((k, v))
        return cnt

    def emit(self, nc, es, final_eng="sp"):
        cnt = self.finalize()
        esem = {e: es.enter_context(nc.semaphore("sem_" + e)) for e in ENGS}
        dsem = {}
        for i, k in enumerate(self.dma_count):
            dsem[k] = es.enter_context(nc.semaphore("dsem_%d" % i))
        block = es.enter_context(nc.Block())
        handles = {"pe": block.tensor, "act": block.scalar, "dve": block.vector,
                   "pool": block.gpsimd, "sp": block.sync}

        def make(e):
            def body(eng):
                for op in self.per_eng[e]:
                    for (k, v) in op.waits:
                        if k[0] == "d":
                            eng.wait_ge(dsem[k[1]], v)
                        else:
                            eng.wait_ge(esem[k[1]], v)
                    name, a, k = op.fn
                    ins = getattr(eng, name)(*a, **k)
                    if op.dma:
                        ins.then_inc(dsem[op.dkey], 16)
                    elif op.sigval is not None:
                        ins.then_inc(esem[e], 1)
                if e == final_eng:
                    for k, c in self.dma_count.items():
                        eng.wait_ge(dsem[k], c)
                    for e2 in ENGS:
                        if cnt[e2] > 0:
                            eng.wait_ge(esem[e2], cnt[e2])
            return body

        for e in ENGS:
            handles[e](make(e))


class Arena:
    def __init__(self, nc, base, limit):
        self.nc = nc
        self.base = base
        self.off = base
        self.limit = limit
        self.n = 0

    def alloc(self, name, shape, dt):
        per = 1
        for s in shape[1:]:
            per *= s
        nbytes = per * (2 if dt == BF16 else 4)
        nbytes = (nbytes + 63) // 64 * 64
        assert self.off + nbytes <= self.limit, (name, self.off, nbytes, self.limit)
        self.n += 1
        t = self.nc.alloc_sbuf_tensor_at("%s_%d" % (name, self.n), list(shape), dt, offset=self.off)
        self.off += nbytes
        return t

    def mark(self):
        return self.off

    def reset(self, off):
        self.off = off


def build(debug=False):
    nc = bass.Bass("TRN2", target_bir_lowering=False)
    P = Prog()

    def din(name, shape):
        return nc.dram_tensor(name, list(shape), F32, kind="ExternalInput").ap()

    xTa = din("xTa", [D, S])
    xTo = din("xTo", [D, NOWN])
    xTh = din("xTh", [D, 128])
    pTo = din("pTo", [256, NOWN])
    w_in = din("w_in", [D, 2048])
    w_pool = din("w_pool", [512, 128])
    w_out = din("w_out", [D, D])
    w_up = din("w_up", [D, 4096])
    w_down = din("w_down", [4096, D])
    w_proj = din("w_proj", [256, D])
    w_gate = din("w_gate", [D, D])
    gv_d = din("gv", [128, 40])
    b31_d = din("b31", [128, 8])
    BTd = din("BTd", [8, 128, 4096])
    msk_d = din("msk", [128, 3 * 8 * 32])
    invfix_d = din("invfix", [128, 64])
    ind_d = din("ind", [32, S])
    ident_d = din("ident", [128, 128])
    outT = nc.dram_tensor("outT", [D, NOWN], F32, kind="ExternalOutput").ap()

    skind = "ExternalOutput" if debug else "Internal"
    KTd = nc.dram_tensor("KTd", [512, S], BF16, kind=skind).ap()
    Vd2 = nc.dram_tensor("Vd2", [8, 128, 64, VS], BF16, kind=skind).ap()
    kbd = nc.dram_tensor("kbd", [512, 32], F32, kind=skind).ap()
    Yd = nc.dram_tensor("Yd", [D, NOWN], BF16, kind=skind).ap()
    Wup_s = nc.dram_tensor("Wup_s", [8, 128, 8, 512], BF16, kind="Internal").ap()
    Wdn_s = nc.dram_tensor("Wdn_s", [8, 128, 4, 1024], BF16, kind="Internal").ap()
    if debug:
        QAd = nc.dram_tensor("QAd", [8, 96, NOWN], BF16, kind="ExternalOutput").ap()
        dWin = nc.dram_tensor("dWin", [128, 8, 2048], BF16, kind="ExternalOutput").ap()
        dxb = nc.dram_tensor("dxb", [128, 8, 512], BF16, kind="ExternalOutput").ap()
        dsq = nc.dram_tensor("dsq", [128, 8, 512], BF16, kind="ExternalOutput").ap()
        drstd = nc.dram_tensor("drstd", [128, 512], F32, kind="ExternalOutput").ap()
        drcol = nc.dram_tensor("drcol", [128, 4], F32, kind="ExternalOutput").ap()
        dktmp = nc.dram_tensor("dktmp", [128, 512], F32, kind="ExternalOutput").ap()
        dmix = nc.dram_tensor("dmix", [128, 8, 512], F32, kind="ExternalOutput").ap()
        dh1 = nc.dram_tensor("dh1", [128, 8, 512], F32, kind="ExternalOutput").ap()
        dh2 = nc.dram_tensor("dh2", [128, 8, 512], F32, kind="ExternalOutput").ap()
        duT = nc.dram_tensor("duT", [128, 32, 512], BF16, kind="ExternalOutput").ap()
        dfT = nc.dram_tensor("dfT", [128, 8, 512], F32, kind="ExternalOutput").ap()

    with ExitStack() as es:
        AR = Arena(nc, SB_BASE, nc.SBUF_PARTITION_SIZE_BYTES)
        PS = [es.enter_context(nc.psum_tensor("psb%d" % b, [128, 512], F32)) for b in range(8)]

        def pk(b):
            return ("ps", b)

        identf = AR.alloc("identf", [128, 128], F32)
        ones_bf = AR.alloc("ones_bf", [128, 128], BF16)
        onesf = AR.alloc("onesf", [128, 64], F32)
        gv = AR.alloc("gv", [128, 40], F32)
        b31 = AR.alloc("b31", [128, 8], F32)
        m_consts = AR.mark()
        QA = AR.alloc("QA", [128, 8, NOWN], BF16)
        m_after_persist = AR.mark()
        G_MIXPRE, G_MIXPOST, G_MLPPRE, G_MLPPOST, G_POOL = 0, 8, 16, 24, 32

        P.dma("sp", lambda e: e.dma_start(out=identf[:], in_=ident_d[:, :]), writes=["identf"])
        P.dma("sp", lambda e: e.dma_start(out=gv[:], in_=gv_d[:, :]), writes=["gv"])
        P.dma("sp", lambda e: e.dma_start(out=b31[:], in_=b31_d[:, :]), writes=["b31"])
        P.op("dve", lambda e: e.memset(ones_bf[:], 1.0), writes=["ones_bf"])
        P.op("dve", lambda e: e.memset(onesf[:], 1.0), writes=["onesf"])

        w_up_v = w_up.rearrange("(c p) (u f) -> u p c f", p=128, f=512)
        w_dn_v = w_down.rearrange("(v q p) n -> v p q n", q=4, p=128)

        WCAST = [("Wup_s", u) for u in range(8)] + [("Wdn_s", v) for v in range(8)]
        Winb = AR.alloc("Winb", [128, 8, 2048], BF16)
        xf = [AR.alloc("xf", [128, 8, 512], F32) for _ in range(2)]
        xb = [AR.alloc("xb", [128, 8, 512], BF16) for _ in range(2)]
        sq = [AR.alloc("sq", [128, 8, 512], BF16) for _ in range(2)]
        sqt = AR.alloc("sqt", [128, 512], F32)
        rstd = [AR.alloc("rstd", [128, 512], F32) for _ in range(2)]
        sqc = AR.alloc("sqc", [128, 4], F32)
        rcol = [AR.alloc("rcol", [128, 4], F32) for _ in range(2)]
        ktmp = [AR.alloc("ktmp", [128, 512], F32) for _ in range(2)]
        kst = [AR.alloc("kst", [128, 4, 512], BF16) for _ in range(2)]
        m_vst = AR.mark()
        vst = [AR.alloc("vst", [128, 8, 4, VS], BF16) for _ in range(2)]
        kbacc = AR.alloc("kbacc", [128, 4, 32], F32)
        for bfi in range(2):
            P.op("dve", lambda e, bfi=bfi: e.memset(vst[bfi][:], 1.0), writes=[("vst", bfi, sub) for sub in range(4)])
        w_in_v = w_in.rearrange("(c p) n -> p c n", p=128)
        for part in (1, 0):
            for c in range(8):
                P.dma("pool", lambda e, c=c, part=part: e.dma_start(out=Winb[:, c, part * 1024:(part + 1) * 1024],
                                                                     in_=w_in_v[:, c, part * 1024:(part + 1) * 1024]),
                      writes=[("Winb", c, part)])
        WINB = [("Winb", c, part) for c in range(8) for part in range(2)]

        XF = lambda bi: [("xf", bi, 0), ("xf", bi, 1)]

        def issue_load(src_ap, n, bi):
            for hlf in range(2):
                P.dma("sp", lambda e, hlf=hlf: e.dma_start(out=xf[bi][:, 4 * hlf:4 * hlf + 4, 0:n],
                                                            in_=src_ap[:, 4 * hlf:4 * hlf + 4, :]),
                      writes=[("xf", bi, hlf)])

        def pre(n, bi):
            P.op("act", lambda e: e.activation(out=sq[bi][:, :, 0:n], in_=xf[bi][:, :, 0:n], func=AF.Square),
                 reads=XF(bi), writes=[("sq", bi)])
            for c in range(8):
                P.op("dve", lambda e, c=c: e.tensor_scalar(out=xb[bi][:, c, 0:n], in0=xf[bi][:, c, 0:n],
                                                            scalar1=gv[:, G_MIXPRE + c:G_MIXPRE + c + 1], scalar2=None,
                                                            op0=ALU.mult),
                     reads=XF(bi) + ["gv"], writes=[("xb", bi, c)])

        def pre_b(n, bi):
            P.op("act", lambda e: e.activation(out=sq[bi][:, :, 0:n], in_=xf[bi][:, :, 0:n], func=AF.Square),
                 reads=XF(bi), writes=[("sq", bi)])
            for c in range(8):
                P.op("act", lambda e, c=c: e.activation(out=xb[bi][:, c, 0:n], in_=xf[bi][:, c, 0:n], func=AF.Copy,
                                                        scale=gv[:, G_MIXPRE + c:G_MIXPRE + c + 1]),
                     reads=XF(bi) + ["gv"], writes=[("xb", bi, c)])

        def post(n, bi):
            for c in range(8):
                P.op("pe", lambda e, c=c: e.matmul(PS[0][:, 0:n], lhsT=ones_bf[:], rhs=sq[bi][:, c, 0:n],
                                                   start=(c == 0), stop=(c == 7)),
                     reads=[("sq", bi), "ones_bf"], writes=[pk(0)])
            P.op("act", lambda e: e.activation(out=sqt[:, 0:n], in_=PS[0][:, 0:n], func=AF.Ln,
                                               scale=1.0 / D, bias=EPS),
                 reads=[pk(0)], writes=["sqt"])
            P.op("act", lambda e: e.activation(out=rstd[bi][:, 0:n], in_=sqt[:, 0:n], func=AF.Exp, scale=-0.5),
                 reads=["sqt"], writes=[("rstd", bi)])

        def norm(n, bi):
            pre(n, bi)
            post(n, bi)

        xTa_v = xTa.rearrange("(c p) t -> p c t", p=128)
        xTh_v = xTh.rearrange("(c p) t -> p c t", p=128)
        xTo_v = xTo.rearrange("(c p) t -> p c t", p=128)
        issue_load(xTa_v[:, :, 0:512], 512, 0)
        issue_load(xTa_v[:, :, 512:1024], 512, 1)
        pre(512, 0)
        for T in range(16):
            bi = T % 2
            post(512, bi)
            if debug and T == 0:
                P.dma("sp", lambda e: e.dma_start(out=dWin, in_=Winb[:]), reads=WINB, writes=["dWin"])
                P.dma("sp", lambda e: e.dma_start(out=dxb, in_=xb[0][:]), reads=[("xb", 0, c) for c in range(8)], writes=["dxb"])
                P.dma("sp", lambda e: e.dma_start(out=dsq, in_=sq[0][:]), reads=[("sq", 0)], writes=["dsq"])
                P.dma("sp", lambda e: e.dma_start(out=drstd, in_=rstd[0][:]), reads=[("rstd", 0)], writes=["drstd"])
            for sub in range(4):
                for c in range(8):
                    P.op("pe", lambda e, c=c, sub=sub: e.matmul(
                        PS[1][:, sub:sub + 1], lhsT=sq[bi][:, c, sub * 128:(sub + 1) * 128], rhs=ones_bf[:, 0:1],
                        start=(c == 0), stop=(c == 7)), reads=[("sq", bi), "ones_bf"], writes=[pk(1)])
            if T + 1 < 16:
                pre(512, (T + 1) % 2)
            else:
                pre(128, 0)
            if T + 2 < 16:
                issue_load(xTa_v[:, :, (T + 2) * 512:(T + 3) * 512], 512, bi)
            elif T + 2 == 16:
                issue_load(xTh_v, 128, 0)
            P.op("act", lambda e: e.activation(out=sqc[:], in_=PS[1][:, 0:4], func=AF.Ln, scale=1.0 / D, bias=EPS),
                 reads=[pk(1)], writes=["sqc"])
            P.op("act", lambda e, bi=bi: e.activation(out=rcol[bi][:], in_=sqc[:], func=AF.Exp, scale=-0.5),
                 reads=["sqc"], writes=[("rcol", bi)])
            for hp in range(4):
                pb = (2, 3, 6)[(4 * T + hp) % 3]
                for c in range(8):
                    P.op("pe", lambda e, c=c, hp=hp, pb=pb: e.matmul(
                        PS[pb][:], lhsT=Winb[:, c, 1024 + hp * 128:1024 + (hp + 1) * 128], rhs=xb[bi][:, c, :],
                        start=(c == 0), stop=(c == 7)), reads=[("Winb", c, 1), ("xb", bi, c)], writes=[pk(pb)])
                P.op("dve", lambda e, hp=hp, pb=pb: e.tensor_tensor(out=ktmp[hp % 2][:], in0=PS[pb][:], in1=rstd[bi][:],
                                                                     op=ALU.mult),
                     reads=[pk(pb), ("rstd", bi)], writes=[("ktmp", hp % 2)])
                if debug and T == 0 and hp == 0:
                    P.dma("sp", lambda e: e.dma_start(out=dktmp, in_=ktmp[0][:]), reads=[("ktmp", 0)], writes=["dktmp"])
                    P.dma("sp", lambda e: e.dma_start(out=drcol, in_=rcol[0][:]), reads=[("rcol", 0)], writes=["drcol"])
                for blk in range(2):
                    P.op("act", lambda e, hp=hp, blk=blk: e.activation(
                        out=kst[bi][:, hp, blk * 256:(blk + 1) * 256], in_=ktmp[hp % 2][:, blk * 256:(blk + 1) * 256],
                        func=AF.Copy, accum_out=kbacc[:, hp, 2 * T + blk:2 * T + blk + 1]),
                        reads=[("ktmp", hp % 2)], writes=[("kst", bi, hp), ("kbacc", hp)])
            P.dma("sp", lambda e, T=T, bi=bi: e.dma_start(
                out=KTd.rearrange("(hp p) t -> p hp t", p=128)[:, :, T * 512:(T + 1) * 512], in_=kst[bi][:]),
                reads=[("kst", bi, hp) for hp in range(4)], writes=[("KTd", T)], key=("KTd", T % 2))
            for sub in range(4):
                pb = (4, 5, 7)[(4 * T + sub) % 3]
                for c in range(8):
                    P.op("pe", lambda e, c=c, sub=sub, pb=pb: e.matmul(
                        PS[pb][:], lhsT=xb[bi][:, c, sub * 128:(sub + 1) * 128], rhs=Winb[:, c, 1536:2048],
                        start=(c == 0), stop=(c == 7)), reads=[("Winb", c, 1), ("xb", bi, c)], writes=[pk(pb)])
                P.op("act", lambda e, sub=sub, pb=pb: e.activation(
                    out=vst[bi][:, :, sub, 0:64], in_=PS[pb][:].rearrange("p (h d) -> p h d", d=64), func=AF.Copy,
                    scale=rcol[bi][:, sub:sub + 1]), reads=[pk(pb), ("rcol", bi)], writes=[("vst", bi, sub)])
            P.dma("sp", lambda e, T=T, bi=bi: e.dma_start(
                out=Vd2.rearrange("h p kt d -> p h (kt d)")[:, :, 4 * T * VS:(4 * T + 4) * VS],
                in_=vst[bi][:].rearrange("p h s d -> p h (s d)")),
                reads=[("vst", bi, sub) for sub in range(4)], writes=[("Vd2", T)], key=("Vd2", T % 2))
        P.op("dve", lambda e: e.tensor_scalar(out=kbacc[:], in0=kbacc[:], scalar1=1.0 / 256, scalar2=None, op0=ALU.mult),
             reads=[("kbacc", hp) for hp in range(4)], writes=[("kbacc", hp) for hp in range(4)])
        P.dma("sp", lambda e: e.dma_start(out=kbd.rearrange("(hp p) n -> p hp n", p=128), in_=kbacc[:]),
              reads=[("kbacc", hp) for hp in range(4)], writes=["kbd"])

        mB = AR.mark()
        zph = AR.alloc("zph", [128, 4, 128], F32)
        zp0 = AR.alloc("zp", [128, 4, 2, 272], F32)
        m_tmpz = AR.mark()
        AR.reset(m_vst)
        zp1 = AR.alloc("zp", [128, 4, 2, 272], F32)
        assert AR.mark() <= m_vst + 2 * 8 * 4 * VS * 2
        AR.reset(m_tmpz)
        zp = [zp0, zp1]
        VSTK = [("vst", b_, s_) for b_ in range(2) for s_ in range(4)]
        sA = AR.alloc("sA", [128, 2, 272], F32)
        sB = AR.alloc("sB", [128, 2, 272], F32)
        dfix = AR.alloc("dfix", [128, 16], F32)
        dbf = [AR.alloc("dbf", [128, 2, 256], BF16) for _ in range(4)]
        ypst = [AR.alloc("ypst", [128, 512], BF16) for _ in range(2)]
        Wpoolb = AR.alloc("Wpoolb", [128, 4, 128], BF16)
        msk = AR.alloc("msk", [128, 3, 8, 32], F32)
        invfix = AR.alloc("invfix", [128, 4, 16], F32)
        kbTf = AR.alloc("kbTf", [64, 8, 32], F32)
        kbTb = AR.alloc("kbTb", [64, 8, 32], BF16)
        msc2 = [AR.alloc("msc2", [128, 16, 32], F32) for _ in range(2)]
        sel2 = [AR.alloc("sel2", [128, 16, 32], F32) for _ in range(2)]
        Mt2 = [AR.alloc("Mt2", [128, 16, 96], F32) for _ in range(2)]
        top8b = [AR.alloc("top8b", [128, 16, 8], F32) for _ in range(2)]

        P.dma("pool", lambda e: e.dma_start(out=Wpoolb[:], in_=w_pool.rearrange("(g p) n -> p g n", p=128)),
              writes=["Wpoolb"])
        P.dma("sp", lambda e: e.dma_start(out=msk[:], in_=msk_d.rearrange("p (a i n) -> p a i n", a=3, i=8)),
              writes=["msk"])
        P.dma("sp", lambda e: e.dma_start(out=invfix[:], in_=invfix_d.rearrange("p (g r) -> p g r", g=4)),
              writes=["invfix"])
        P.dma("sp", lambda e: e.dma_start(out=kbTf[:], in_=kbd.rearrange("(h d) n -> d h n", d=64)),
              reads=["kbd"], writes=["kbTf"])
        P.op("dve", lambda e: e.tensor_copy(out=kbTb[:], in_=kbTf[:]), reads=["kbTf"], writes=["kbTb"])
        for bk in range(2):
            P.op("dve", lambda e, bk=bk: e.memset(Mt2[bk][:], 0.0), writes=[("Mt2", bk)])

        issue_load(xTo_v[:, :, 0:512], 512, 1)
        post(128, 0)
        for g in range(4):
            pb = 2 + g % 2
            for c in range(8):
                P.op("pe", lambda e, c=c, g=g, pb=pb: e.matmul(
                    PS[pb][:, 0:128], lhsT=Winb[:, c, g * 128:(g + 1) * 128], rhs=xb[0][:, c, 0:128],
                    start=(c == 0), stop=(c == 7)), reads=[("Winb", c, 0), ("xb", 0, c)], writes=[pk(pb)])
            P.op("dve", lambda e, g=g, pb=pb: e.tensor_tensor(out=zph[:, g, :], in0=PS[pb][:, 0:128],
                                                               in1=rstd[0][:, 0:128], op=ALU.mult),
                 reads=[pk(pb), ("rstd", 0)], writes=["zph"])

        def b_norm_q(t, bi):
            post(512, bi)
            for h in range(8):
                pb = 2 + h % 2
                for c in range(8):
                    P.op("pe", lambda e, c=c, h=h, pb=pb: e.matmul(
                        PS[pb][0:64, :], lhsT=Winb[:, c, 512 + h * 64:512 + (h + 1) * 64], rhs=xb[bi][:, c, :],
                        start=(c == 0), stop=(c == 7)), reads=[("Winb", c, 0), ("xb", bi, c)], writes=[pk(pb)])
                P.op("dve", lambda e, h=h, pb=pb: e.scalar_tensor_tensor(
                    out=QA[0:64, h, t * 512:(t + 1) * 512], in0=PS[pb][0:64, :], scalar=0.125, in1=rstd[bi][0:64, :],
                    op0=ALU.mult, op1=ALU.mult), reads=[pk(pb), ("rstd", bi)], writes=[("QAq", h, t)])

        def b_z(t, bi):
            for g in range(4):
                pb = 4 + g % 2
                for c in range(8):
                    P.op("pe", lambda e, c=c, g=g, pb=pb: e.matmul(
                        PS[pb][:], lhsT=Winb[:, c, g * 128:(g + 1) * 128], rhs=xb[bi][:, c, :],
                        start=(c == 0), stop=(c == 7)), reads=[("Winb", c, 0), ("xb", bi, c)], writes=[pk(pb)])
                P.op("dve", lambda e, g=g, pb=pb: e.tensor_tensor(
                    out=zp[t % 2][:, g, :, 16:272], in0=PS[pb][:].rearrange("p (b r) -> p b r", b=2),
                    in1=rstd[bi][:].rearrange("p (b r) -> p b r", b=2), op=ALU.mult),
                    reads=[pk(pb), ("rstd", bi)], writes=[("zp", t % 2, g)] + (VSTK if t % 2 else []))
                P.op("pool", lambda e, g=g: e.tensor_copy(
                    out=zp[t % 2][:, g, :, 0:16], in_=zph[:, g, 32 * t:32 * t + 32].rearrange("p (b r) -> p b r", b=2)),
                    reads=["zph", ("zp", t % 2, g)], writes=[("zp", t % 2, g)])

        def b_mixer(t):
            for g in range(4):
                w = 2 << g
                src = zp[t % 2][:, g]
                bufs = [sA, sB]
                cur = None
                sh = 1
                for k in range(g + 1):
                    dst = bufs[k % 2]
                    lo = 2 * sh - 1
                    a = src if cur is None else cur
                    P.op("dve", lambda e, a=a, dst=dst, lo=lo, sh=sh: e.tensor_tensor(
                        out=dst[:, :, lo:272], in0=a[:, :, lo:272], in1=a[:, :, lo - sh:272 - sh], op=ALU.add),
                        reads=[("zp", t % 2, g), "sA", "sB"], writes=["sA" if k % 2 == 0 else "sB"])
                    cur = dst
                    sh *= 2
                db = dbf[g]
                P.op("dve", lambda e, cur=cur, src=src, db=db, w=w: e.scalar_tensor_tensor(
                    out=db[:], in0=cur[:, :, 16:272], scalar=1.0 / w, in1=src[:, :, 16:272],
                    op0=ALU.mult, op1=ALU.subtract), reads=["sA", "sB", ("zp", t % 2, g)], writes=[("dbf", g)])
                if t == 0:
                    P.op("dve", lambda e, cur=cur, g=g: e.tensor_tensor(out=dfix[:], in0=cur[:, 0, 16:32],
                                                                         in1=invfix[:, g, :], op=ALU.mult),
                         reads=["sA", "sB", "invfix"], writes=["dfix"])
                    P.op("dve", lambda e, src=src, db=db: e.tensor_tensor(out=db[:, 0, 0:16], in0=dfix[:],
                                                                           in1=src[:, 0, 16:32], op=ALU.subtract),
                         reads=["dfix", ("zp", t % 2, g), ("dbf", g)], writes=[("dbf", g)])

        def b_poolout(t):
            for g in range(4):
                db = dbf[g]
                P.op("pe", lambda e, g=g, db=db: e.matmul(PS[6][:], lhsT=Wpoolb[:, g, :],
                                                          rhs=db[:].rearrange("p b r -> p (b r)"),
                                                          start=True, stop=True),
                     reads=["Wpoolb", ("dbf", g)], writes=[pk(6)])
                P.op("act", lambda e, g=g: e.activation(out=ypst[g % 2][:], in_=PS[6][:], func=AF.Copy,
                                                        scale=gv[:, G_POOL + g:G_POOL + g + 1]),
                     reads=[pk(6), "gv"], writes=[("ypst", g % 2)])
                P.dma("sp", lambda e, g=g: e.dma_start(out=Yd[g * 128:(g + 1) * 128, t * 512:(t + 1) * 512],
                                                        in_=ypst[g % 2][:]),
                      reads=[("ypst", g % 2)], writes=[("Yd", g, t)], key=("Ydp", g % 2))

        def b_scores(t):
            for qq in range(4):
                qt = 4 * t + qq
                bank = 7 if qq < 2 else 1
                for h in range(8):
                    col = (qq % 2) * 256 + h * 32
                    P.op("pe", lambda e, h=h, qt=qt, bank=bank, col=col: e.matmul(
                        PS[bank][:, col:col + 32], lhsT=QA[0:64, h, qt * 128:(qt + 1) * 128], rhs=kbTb[:, h, :],
                        start=True, stop=True), reads=[("QAq", h, t), "kbTb"], writes=[pk(bank)])

        def b_sel(t):
            for bk in range(2):
                bank = 7 if bk == 0 else 1
                iblk = 2 * t + bk
                bshape = [128, 16, 32]
                P.op("dve", lambda e, bk=bk, bank=bank, iblk=iblk: e.tensor_tensor(
                    out=msc2[bk][:], in0=PS[bank][:].rearrange("p (a n) -> p a n", n=32),
                    in1=msk[:, 0, iblk:iblk + 1, :].broadcast_to(bshape), op=ALU.add),
                    reads=[pk(bank), "msk"], writes=[("msc2", bk)])
                for a in range(16):
                    P.op("dve", lambda e, bk=bk, a=a: e.max(out=top8b[bk][:, a, :], in_=msc2[bk][:, a, :]),
                         reads=[("msc2", bk)], writes=[("top8b", bk, a)])
                P.op("dve", lambda e, bk=bk: e.tensor_tensor(out=sel2[bk][:], in0=msc2[bk][:],
                                                             in1=top8b[bk][:, :, 2:3].broadcast_to(bshape), op=ALU.is_ge),
                     reads=[("msc2", bk)] + [("top8b", bk, a) for a in range(16)], writes=[("sel2", bk)])
                P.op("dve", lambda e, bk=bk, iblk=iblk: e.tensor_tensor(out=sel2[bk][:], in0=sel2[bk][:],
                                                                         in1=msk[:, 1, iblk:iblk + 1, :].broadcast_to(bshape),
                                                                         op=ALU.mult),
                     reads=[("sel2", bk), "msk"], writes=[("sel2", bk)])
                P.op("dve", lambda e, bk=bk, iblk=iblk: e.tensor_tensor(out=sel2[bk][:], in0=sel2[bk][:],
                                                                         in1=msk[:, 2, iblk:iblk + 1, :].broadcast_to(bshape),
                                                                         op=ALU.max),
                     reads=[("sel2", bk), "msk"], writes=[("sel2", bk)])
                P.op("dve", lambda e, bk=bk: e.tensor_scalar(out=Mt2[bk][:, :, 64:96], in0=sel2[bk][:], scalar1=MASKV,
                                                             scalar2=-MASKV, op0=ALU.mult, op1=ALU.add),
                     reads=[("sel2", bk)], writes=[("Mt2", bk)])

        def b_transposes(t):
            for bk in range(2):
                for qtl in range(2):
                    qt = 4 * t + 2 * bk + qtl
                    for hh in range(2):
                        pb = 2 + 2 * qtl + hh
                        for h4 in range(4):
                            h = hh * 4 + h4
                            P.op("pe", lambda e, h=h, h4=h4, pb=pb, bk=bk, qtl=qtl: e.transpose(
                                PS[pb][0:96, h4 * 128:(h4 + 1) * 128], Mt2[bk][:, qtl * 8 + h, :], identf[:]),
                                reads=[("Mt2", bk), "identf"], writes=[pk(pb)])
                        P.op("act", lambda e, hh=hh, pb=pb, qt=qt: e.activation(
                            out=QA[64:96, hh * 4:hh * 4 + 4, qt * 128:(qt + 1) * 128],
                            in_=PS[pb][64:96, :].rearrange("p (h q) -> p h q", h=4), func=AF.Copy),
                            reads=[pk(pb)], writes=[("QAm", hh, t)])

        pre_b(512, 1)
        issue_load(xTo_v[:, :, 512:1024], 512, 0)
        for t in range(4):
            bi = (t + 1) % 2
            b_norm_q(t, bi)
            b_scores(t)
            b_z(t, bi)
            if t > 0:
                b_mixer(t - 1)
            if t + 1 < 4:
                pre_b(512, t % 2)
            if t + 2 < 4:
                issue_load(xTo_v[:, :, (t + 2) * 512:(t + 3) * 512], 512, bi)
            if t > 0:
                b_transposes(t - 1)
                b_poolout(t - 1)
            b_sel(t)
        b_mixer(3)
        b_transposes(3)
        b_poolout(3)
        if debug:
            for h in range(8):
                P.dma("sp", lambda e, h=h: e.dma_start(out=QAd[h], in_=QA[0:96, h, :]),
                      reads=[("QAq", h, t) for t in range(4)] + [("QAm", h // 4, t) for t in range(4)],
                      writes=[("QAd", h)], key="QAd")

        AR.reset(m_after_persist)
        KA = [AR.alloc("KA", [128, S], BF16) for _ in range(2)]
        VA = [AR.alloc("VA", [128, 64 * VS + 64], BF16) for _ in range(2)]
        BT = [AR.alloc("BT", [128, 8, 512], F32) for _ in range(2)]
        PT = [AR.alloc("PT", [128, 512], BF16) for _ in range(6)]
        tS = [AR.alloc("tS", [128, 512], F32) for _ in range(4)]
        Osb = [AR.alloc("Osb", [128, 256], F32) for _ in range(2)]
        rrow = [AR.alloc("rrow", [128, 256], F32) for _ in range(2)]
        yst = [AR.alloc("yst", [64, 256], BF16) for _ in range(2)]
        m_wstart = AR.mark()
        Woutb = AR.alloc("Woutb", [128, 8, 1024], BF16)
        Wgateb = AR.alloc("Wgateb", [128, 8, 1024], BF16)
        Wprojb = AR.alloc("Wprojb", [128, 2, 1024], BF16)
        Wupb = [AR.alloc("Wupb", [128, 8, 512], BF16) for _ in range(2)]
        Wdnb = [AR.alloc("Wdnb", [128, 4, 1024], BF16) for _ in range(2)]
        newk = ["Woutb", "Wgateb", "Wprojb", ("Wupb", 0), ("Wupb", 1), ("Wdnb", 0), ("Wdnb", 1)] + \
               [("KA", i, hl) for i in range(2) for hl in range(2)] + [("KAi", i) for i in range(2)] + \
               [("VA", i) for i in range(2)] + [("VApad", i) for i in range(2)] + [("BT", i, hl) for i in range(2) for hl in range(2)] + [("PT", i) for i in range(6)] + [("tS", i) for i in range(4)] + \
               [("Osb", i) for i in range(2)] + [("rrow", i) for i in range(2)] + [("yst", i) for i in range(2)]
        P.barrier(newk)
        for bfi in range(2):
            P.op("dve", lambda e, bfi=bfi: e.memset(VA[bfi][:, 64 * VS:64 * VS + 64], 0.0), writes=[("VApad", bfi)])
            P.dma("pool", lambda e, bfi=bfi: e.dma_start(out=KA[bfi][64:96, :].rearrange("n (a b) -> n a b", b=1024),
                                                         in_=ind_d.rearrange("n (a b) -> n a b", b=1024)), writes=[("KAi", bfi)])

        for u in range(8):
            P.dma("pool", lambda e, u=u: e.dma_start(out=Wup_s[u], in_=w_up_v[u]), writes=[("Wup_s", u)])
        for v in range(8):
            P.dma("pool", lambda e, v=v: e.dma_start(out=Wdn_s[v], in_=w_dn_v[v]), writes=[("Wdn_s", v)])

        P.dma("pool", lambda e: e.dma_start(out=Woutb[:], in_=w_out.rearrange("(c p) n -> p c n", p=128)), writes=["Woutb"])
        P.dma("pool", lambda e: e.dma_start(out=Wgateb[:], in_=w_gate.rearrange("(c p) n -> p c n", p=128)), writes=["Wgateb"])
        P.dma("pool", lambda e: e.dma_start(out=Wprojb[:], in_=w_proj.rearrange("(c p) n -> p c n", p=128)), writes=["Wprojb"])

        def load_head(h):
            bfi = h % 2
            for hlf in range(2):
                P.dma("sp", lambda e, hlf=hlf: e.dma_start(out=KA[bfi][0:64, hlf * 4096:(hlf + 1) * 4096],
                                                            in_=KTd[h * 64:(h + 1) * 64, hlf * 4096:(hlf + 1) * 4096]),
                      reads=[("KTd", T) for T in range(16)], writes=[("KA", bfi, hlf)])
            P.dma("sp", lambda e: e.dma_start(out=VA[bfi][:, 0:64 * VS], in_=Vd2[h].rearrange("p k d -> p (k d)")),
                  reads=[("Vd2", T) for T in range(16)], writes=[("VA", bfi)])
            for hlf in range(2):
                P.dma("sp", lambda e, hlf=hlf: e.dma_start(out=BT[bfi][:, 4 * hlf:4 * hlf + 4, :],
                                                            in_=BTd[h, :, 2048 * hlf:2048 * (hlf + 1)].rearrange(
                                                                "p (s x) -> p s x", s=4)),
                      writes=[("BT", bfi, hlf)])

        def shift_bias(h):
            bfi = h % 2
            for hlf in range(2):
                P.op("dve", lambda e, hlf=hlf: e.tensor_scalar(out=BT[bfi][:, 4 * hlf:4 * hlf + 4, :],
                                                                in0=BT[bfi][:, 4 * hlf:4 * hlf + 4, :],
                                                                scalar1=b31[:, h:h + 1], scalar2=None, op0=ALU.subtract),
                     reads=[("BT", bfi, hlf), "b31"], writes=[("BT", bfi, hlf)])

        pairs = []
        for h in range(8):
            for i in range(8):
                ns = list(range(0, 4 * i + 4))
                for idx, n in enumerate(ns):
                    pairs.append((h, i, n, idx == 0, idx == len(ns) - 1))

        def emit_S(pi):
            h, i, n, first, last = pairs[pi]
            bfi = h % 2
            sb = SBANK[pi % 5]
            for hf in range(2):
                P.op("pe", lambda e, hf=hf: e.matmul(PS[sb][:, hf * 256:(hf + 1) * 256],
                                                     lhsT=KA[bfi][0:96, n * 256 + hf * 128:n * 256 + (hf + 1) * 128],
                                                     rhs=QA[0:96, h, i * 256:(i + 1) * 256], start=True, stop=True),
                     reads=[("KA", bfi, 0), ("KA", bfi, 1), ("KAi", bfi), ("QAq", h, i // 2), ("QAm", h // 4, i // 2)], writes=[pk(sb)])
            near = n >= 4 * i - 4
            pt = pi % 6
            if near:
                s = n - (4 * i - 4)
                ts = pi % 4
                P.op("dve", lambda e: e.tensor_tensor(out=tS[ts][:], in0=PS[sb][:], in1=BT[bfi][:, s, :], op=ALU.add),
                     reads=[pk(sb), ("BT", bfi, 0), ("BT", bfi, 1)], writes=[("tS", ts)])
                P.op("act", lambda e: e.activation(out=PT[pt][:], in_=tS[ts][:], func=AF.Exp),
                     reads=[("tS", ts)], writes=[("PT", pt)])
            else:
                P.op("act", lambda e: e.activation(out=PT[pt][:], in_=PS[sb][:], func=AF.Exp),
                     reads=[pk(sb)], writes=[("PT", pt)])

        def emit_PV(pi, extra=()):
            h, i, n, first, last = pairs[pi]
            bfi = h % 2
            pt = pi % 6
            ob = 3 + i % 2
            for hf in range(2):
                P.op("pe", lambda e, hf=hf: e.matmul(PS[ob][:, 0:256],
                                                     lhsT=VA[bfi][:, (2 * n + hf) * VS:(2 * n + hf) * VS + 128],
                                                     rhs=PT[pt][:, hf * 256:(hf + 1) * 256],
                                                     start=(first and hf == 0), stop=(last and hf == 1)),
                     reads=[("VA", bfi), ("VApad", bfi), ("PT", pt)] + list(extra), writes=[pk(ob)])

        def emit_epi1(pi):
            h, i, n, first, last = pairs[pi]
            o = i % 2
            ob = 3 + i % 2
            P.op("act", lambda e: e.activation(out=Osb[o][0:65, :], in_=PS[ob][0:65, 0:256], func=AF.Copy),
                 reads=[pk(ob)], writes=[("Osb", o)])
            P.op("dve", lambda e: e.reciprocal(out=rrow[o][64:65, :], in_=Osb[o][64:65, :]),
                 reads=[("Osb", o)], writes=[("rrow", o)])

        def emit_epi(pi):
            h, i, n, first, last = pairs[pi]
            o = i % 2
            P.op("pe", lambda e: e.matmul(PS[5][0:64, 0:256], lhsT=onesf[64:65, 0:64], rhs=rrow[o][64:65, :],
                                          start=True, stop=True), reads=[("rrow", o), "onesf"], writes=[pk(5)])
            P.op("dve", lambda e: e.tensor_tensor(out=yst[o][:], in0=Osb[o][0:64, :], in1=PS[5][0:64, 0:256],
                                                  op=ALU.mult), reads=[("Osb", o), pk(5)], writes=[("yst", o)])
            P.dma("sp", lambda e: e.dma_start(out=Yd[512 + h * 64:512 + (h + 1) * 64, i * 256:(i + 1) * 256],
                                              in_=yst[o][:]),
                  reads=[("yst", o)], writes=[("Yd", "a", h, i)], key=("Yda", o))

        SBANK = [0, 1, 2, 6, 7]
        load_head(0)
        shift_bias(0)
        LOOK = 4
        EPI1 = 4
        EPI2 = 10
        npairs = len(pairs)

        def after_pv(pv):
            h, i, n, first, last = pairs[pv]
            if first and i == 0 and h + 1 < 8:
                load_head(h + 1)
            if first and i == 4 and h + 1 < 8:
                shift_bias(h + 1)

        for pi in range(0, npairs + LOOK + EPI2 + 2, 2):
            if pi < npairs:
                emit_S(pi)
            pv0, pv1 = pi - LOOK, pi - LOOK + 1
            if 0 <= pv0 < npairs:
                extra = [("PT", pv1 % 6)] if pv1 < npairs else []
                emit_PV(pv0, extra)
                after_pv(pv0)
            if 0 <= pv1 < npairs:
                emit_PV(pv1)
                after_pv(pv1)
            if pi + 1 < npairs:
                emit_S(pi + 1)
            for stp in (pi, pi + 1):
                e1 = stp - LOOK - EPI1
                if 0 <= e1 < npairs and pairs[e1][4]:
                    emit_epi1(e1)
                e2 = stp - LOOK - EPI2
                if 0 <= e2 < npairs and pairs[e2][4]:
                    emit_epi(e2)

        AR.reset(m_consts)
        mixin = AR.alloc("mixin", [128, 8, 512], BF16)
        xo = AR.alloc("xo", [128, 8, 512], F32)
        pf = AR.alloc("pf", [128, 2, 512], F32)
        pbb = AR.alloc("pbb", [128, 2, 512], BF16)
        hT = AR.alloc("hT", [128, 8, 512], F32)
        sq2 = AR.alloc("sq2", [128, 8, 512], BF16)
        hb = AR.alloc("hb", [128, 8, 512], BF16)
        m_uT = AR.mark()
        uT = AR.alloc("uT", [128, 32, 512], BF16)
        m_tmp = AR.mark()
        AR.reset(m_uT)
        mixT = AR.alloc("mixT", [128, 8, 512], F32)
        AR.reset(m_tmp)
        fT = AR.alloc("fT", [128, 8, 512], F32)
        tt = [AR.alloc("tt", [128, 512], F32) for _ in range(2)]
        sqt2 = AR.alloc("sqt2", [128, 512], F32)
        rs2 = AR.alloc("rs2", [128, 512], F32)
        sg = [AR.alloc("sg", [128, 512], F32) for _ in range(2)]
        assert AR.mark() <= m_wstart, (AR.mark(), m_wstart)
        newk = ["mixin", ("xo", 0), ("xo", 1), "pf",
                "pbb", "sqt2", "rs2", ("tt", 0), ("tt", 1), ("sg", 0), ("sg", 1)] + \
               [("hT", m) for m in range(8)] + [("uT", f) for f in range(32)] + [("fT", m) for m in range(8)] + \
               [("sq2", m) for m in range(8)] + [("hb", m) for m in range(8)]
        P.barrier(newk)

        def stats(src_keys):
            for m in range(8):
                P.op("pe", lambda e, m=m: e.matmul(PS[0][:], lhsT=ones_bf[:], rhs=sq2[:, m, :], start=(m == 0), stop=(m == 7)),
                     reads=[("sq2", m), "ones_bf"], writes=[pk(0)])
            P.op("act", lambda e: e.activation(out=sqt2[:], in_=PS[0][:], func=AF.Ln, scale=1.0 / D, bias=EPS),
                 reads=[pk(0)], writes=["sqt2"])
            P.op("act", lambda e: e.activation(out=rs2[:], in_=sqt2[:], func=AF.Exp, scale=-0.5),
                 reads=["sqt2"], writes=["rs2"])

        Yd_v = Yd.rearrange("(c p) t -> p c t", p=128)
        pTo_v = pTo.rearrange("(c p) t -> p c t", p=128)
        outT_v = outT.rearrange("(c p) t -> p c t", p=128)
        yd_keys = [("Yd", g, t) for g in range(4) for t in range(4)] + [("Yd", "a", h, i) for h in range(8) for i in range(8)]
        def d_loads(t):
            tsl = slice(t * 512, (t + 1) * 512)
            P.dma("sp", lambda e: e.dma_start(out=mixin[:], in_=Yd_v[:, :, tsl]), reads=yd_keys, writes=["mixin"])
            for hlf in range(2):
                P.dma("sp", lambda e, hlf=hlf: e.dma_start(out=xo[:, 4 * hlf:4 * hlf + 4, :],
                                                            in_=xTo_v[:, 4 * hlf:4 * hlf + 4, tsl]), writes=[("xo", hlf)])
            P.dma("sp", lambda e: e.dma_start(out=pf[:], in_=pTo_v[:, :, tsl]), writes=["pf"])

        d_loads(0)
        for t in range(4):
            tsl = slice(t * 512, (t + 1) * 512)
            P.op("pool", lambda e: e.tensor_copy(out=pbb[:], in_=pf[:]), reads=["pf"], writes=["pbb"])
            for m in range(8):
                pb = 1 + m % 3
                for c in range(8):
                    P.op("pe", lambda e, c=c, m=m, pb=pb: e.matmul(PS[pb][:], lhsT=Woutb[:, c, m * 128:(m + 1) * 128],
                                                                    rhs=mixin[:, c, :], start=(c == 0), stop=(c == 7)),
                         reads=["Woutb", "mixin"], writes=[pk(pb)])
                P.op("act", lambda e, m=m, pb=pb: e.activation(out=mixT[:, m, :], in_=PS[pb][:], func=AF.Copy),
                     reads=[pk(pb)], writes=[("uT", 2 * m), ("uT", 2 * m + 1)])
                P.op("dve", lambda e, m=m: e.tensor_tensor(out=sq2[:, m, :], in0=mixT[:, m, :], in1=mixT[:, m, :], op=ALU.mult),
                     reads=[("uT", 2 * m), ("uT", 2 * m + 1)], writes=[("sq2", m)])
            if debug and t == 0:
                P.dma("sp", lambda e: e.dma_start(out=dmix, in_=mixT[:]), reads=[("uT", f) for f in range(16)], writes=["dmix"])
            stats(None)
            for m in range(8):
                P.op("dve", lambda e, m=m: e.scalar_tensor_tensor(out=hT[:, m, :], in0=mixT[:, m, :],
                                                                   scalar=gv[:, G_MIXPOST + m:G_MIXPOST + m + 1],
                                                                   in1=rs2[:], op0=ALU.mult, op1=ALU.mult),
                     reads=[("uT", 2 * m), ("uT", 2 * m + 1), "rs2", "gv"], writes=[("hT", m)])
                P.op("dve", lambda e, m=m: e.tensor_tensor(out=hT[:, m, :], in0=hT[:, m, :], in1=xo[:, m, :], op=ALU.add),
                     reads=[("hT", m), ("xo", m // 4)], writes=[("hT", m)])
                P.op("act", lambda e, m=m: e.activation(out=hb[:, m, :], in_=hT[:, m, :], func=AF.Copy,
                                                        scale=gv[:, G_MLPPRE + m:G_MLPPRE + m + 1]),
                     reads=[("hT", m), "gv"], writes=[("hb", m)])
                P.op("act", lambda e, m=m: e.activation(out=sq2[:, m, :], in_=hT[:, m, :], func=AF.Square),
                     reads=[("hT", m)], writes=[("sq2", m)])
            if debug and t == 0:
                P.dma("sp", lambda e: e.dma_start(out=dh1, in_=hT[:]), reads=[("hT", m) for m in range(8)], writes=["dh1"])
            stats(None)
            for u in range(8):
                P.dma("sp", lambda e, u=u: e.dma_start(out=Wupb[u % 2][:], in_=Wup_s[u]),
                      reads=WCAST, writes=[("Wupb", u % 2)])
                for fq in range(4):
                    fc = 4 * u + fq
                    pb = 1 + fc % 3
                    for c in range(8):
                        P.op("pe", lambda e, c=c, fq=fq, pb=pb, u=u: e.matmul(
                            PS[pb][:], lhsT=Wupb[u % 2][:, c, fq * 128:(fq + 1) * 128], rhs=hb[:, c, :],
                            start=(c == 0), stop=(c == 7)), reads=[("Wupb", u % 2), ("hb", c)], writes=[pk(pb)])
                    P.op("dve", lambda e, fc=fc, pb=pb: e.scalar_tensor_tensor(out=tt[fc % 2][:], in0=PS[pb][:], scalar=0.0,
                                                                                in1=rs2[:], op0=ALU.max, op1=ALU.mult),
                         reads=[pk(pb), "rs2"], writes=[("tt", fc % 2)])
                    P.op("act", lambda e, fc=fc: e.activation(out=uT[:, fc, :], in_=tt[fc % 2][:], func=AF.Square),
                         reads=[("tt", fc % 2)], writes=[("uT", fc)])
            if debug and t == 0:
                P.dma("sp", lambda e: e.dma_start(out=duT, in_=uT[:]), reads=[("uT", f) for f in range(32)], writes=["duT"])
            for v in range(8):
                P.dma("sp", lambda e, v=v: e.dma_start(out=Wdnb[v % 2][:], in_=Wdn_s[v]),
                      reads=WCAST, writes=[("Wdnb", v % 2)])
                for fq in range(4):
                    fc = 4 * v + fq
                    for m in range(8):
                        P.op("pe", lambda e, fq=fq, fc=fc, m=m, v=v: e.matmul(
                            PS[m][:], lhsT=Wdnb[v % 2][:, fq, m * 128:(m + 1) * 128], rhs=uT[:, fc, :],
                            start=(fc == 0), stop=(fc == 31)), reads=[("Wdnb", v % 2), ("uT", fc)], writes=[pk(m)])
            for m in range(8):
                P.op("act", lambda e, m=m: e.activation(out=fT[:, m, :], in_=PS[m][:], func=AF.Copy),
                     reads=[pk(m)], writes=[("fT", m)])
                P.op("dve", lambda e, m=m: e.tensor_tensor(out=sq2[:, m, :], in0=fT[:, m, :], in1=fT[:, m, :], op=ALU.mult),
                     reads=[("fT", m)], writes=[("sq2", m)])
            stats(None)
            for m in range(8):
                P.op("dve", lambda e, m=m: e.scalar_tensor_tensor(out=fT[:, m, :], in0=fT[:, m, :],
                                                                   scalar=gv[:, G_MLPPOST + m:G_MLPPOST + m + 1],
                                                                   in1=rs2[:], op0=ALU.mult, op1=ALU.mult),
                     reads=[("fT", m), "rs2", "gv"], writes=[("fT", m)])
                P.op("dve", lambda e, m=m: e.tensor_tensor(out=hT[:, m, :], in0=hT[:, m, :], in1=fT[:, m, :], op=ALU.add),
                     reads=[("hT", m), ("fT", m)], writes=[("hT", m)])
                P.op("act", lambda e, m=m: e.activation(out=hb[:, m, :], in_=hT[:, m, :], func=AF.Copy),
                     reads=[("hT", m)], writes=[("hb", m)])
            if debug and t == 0:
                P.dma("sp", lambda e: e.dma_start(out=dh2, in_=hT[:]), reads=[("hT", m) for m in range(8)], writes=["dh2"])
                P.dma("sp", lambda e: e.dma_start(out=dfT, in_=fT[:]), reads=[("fT", m) for m in range(8)], writes=["dfT"])
            gate_first = True
            for m in range(8):
                pg = 1 + (2 * m) % 6
                pp = 1 + (2 * m + 1) % 6
                for c in range(8):
                    P.op("pe", lambda e, c=c, m=m, pg=pg: e.matmul(PS[pg][:], lhsT=Wgateb[:, c, m * 128:(m + 1) * 128],
                                                                    rhs=hb[:, c, :], start=(c == 0), stop=(c == 7)),
                         reads=["Wgateb", ("hb", c)], writes=[pk(pg)])
                for c in range(2):
                    P.op("pe", lambda e, c=c, m=m, pp=pp: e.matmul(PS[pp][:], lhsT=Wprojb[:, c, m * 128:(m + 1) * 128],
                                                                    rhs=pbb[:, c, :], start=(c == 0), stop=(c == 1)),
                         reads=["Wprojb", "pbb"], writes=[pk(pp)])
                P.op("act", lambda e, m=m, pg=pg: e.activation(out=sg[m % 2][:], in_=PS[pg][:], func=AF.Sigmoid),
                     reads=[pk(pg)], writes=[("sg", m % 2)])
                P.op("dve", lambda e, m=m, pp=pp: e.tensor_tensor(out=sg[m % 2][:], in0=sg[m % 2][:], in1=PS[pp][:],
                                                                   op=ALU.mult),
                     reads=[("sg", m % 2), pk(pp)], writes=[("sg", m % 2)])
                P.op("dve", lambda e, m=m: e.tensor_tensor(out=fT[:, m, :], in0=sg[m % 2][:], in1=hT[:, m, :], op=ALU.add),
                     reads=[("sg", m % 2), ("hT", m), ("fT", m)], writes=[("fT", m)])
            if t + 1 < 4:
                d_loads(t + 1)
            for hlf in range(2):
                P.dma("sp", lambda e, tsl=tsl, hlf=hlf: e.dma_start(out=outT_v[:, 4 * hlf:4 * hlf + 4, tsl],
                                                                     in_=fT[:, 4 * hlf:4 * hlf + 4, :]),
                      reads=[("fT", m) for m in range(4 * hlf, 4 * hlf + 4)], writes=[("out", t, hlf)], key=("out", hlf))

        P.emit(nc, es)
    return nc


def _t5_bucket(d):
    n = np.maximum(d, 0)
    nf = np.maximum(n, 1).astype(np.float32)
    large = 16 + (np.log(nf / np.float32(16)) / np.float32(math.log(1024 / 16)) * np.float32(16)).astype(np.int32)
    large = np.minimum(large, 31)
    return np.where(n < 16, n, large)


def _core_inputs(c, inp, shared):
    b, j = c // 4, c % 4
    x = inp["x"][b]
    own = np.concatenate([np.arange((4 * i + j) * 256, (4 * i + j + 1) * 256) for i in range(8)])
    xT = shared["xT"][b]
    xTo = np.ascontiguousarray(xT[:, own])
    xTh = np.zeros((D, 128), np.float32)
    for i in range(8):
        s0 = (4 * i + j) * 256
        if s0 >= 16:
            xTh[:, i * 16:(i + 1) * 16] = xT[:, s0 - 16:s0]
    pTo = np.ascontiguousarray(inp["p"][0, b][own].T)
    rel_bias = inp["rel_bias"]
    k = np.arange(128)[:, None, None, None]
    s = np.arange(8)[None, :, None, None]
    hf = np.arange(2)[None, None, :, None]
    q = np.arange(256)[None, None, None, :]
    delta = 4 + j - s
    dist = delta * 256 + q - (hf * 128 + k)
    bucket = _t5_bucket(dist)
    BTt = np.empty((8, 128, 8, 2, 256), np.float32)
    for h in range(8):
        tb = rel_bias[:, h][bucket]
        tb = np.where(dist < 0, np.float32(-MASKV), tb)
        tb = np.where(np.broadcast_to(delta, tb.shape) < 0, np.float32(0.0), tb)
        BTt[h] = tb
    msk = np.zeros((3, 8, 32), np.float32)
    for i in range(8):
        g = 4 * i + j
        n = np.arange(32)
        msk[0, i] = np.where(n < g, 0.0, -1e30)
        msk[1, i] = (n < g)
        msk[2, i] = (n == g)
    invfix = np.zeros((4, 16), np.float32)
    for g in range(4):
        w = 2 << g
        if j == 0:
            invfix[g] = 1.0 / np.minimum(np.arange(16) + 1, w)
        else:
            invfix[g] = 1.0 / w
    d = dict(shared["common"])
    d.update({
        "xTa": xT, "xTo": xTo, "xTh": xTh, "pTo": pTo,
        "BTd": BTt.reshape(8, 128, 4096),
        "msk": np.ascontiguousarray(np.broadcast_to(msk.reshape(1, -1), (128, 768))),
        "invfix": np.ascontiguousarray(np.broadcast_to(invfix.reshape(1, -1), (128, 64))),
        "b31": np.ascontiguousarray(np.broadcast_to(rel_bias[31][None, :], (128, 8))),
    })
    return d, own


def _prep(inp):
    inp = {k: np.asarray(v, dtype=np.float32) for k, v in inp.items()}
    cols = lambda g: np.ascontiguousarray(g.reshape(-1, 128).T)
    gvv = np.zeros((128, 40), np.float32)
    gvv[:, 0:8] = cols(inp["g_mix_pre"][0])
    gvv[:, 8:16] = cols(inp["g_mix_post"][0])
    gvv[:, 16:24] = cols(inp["g_mlp_pre"][0])
    gvv[:, 24:32] = cols(inp["g_mlp_post"][0])
    gvv[:, 32:36] = cols(inp["pool_scale"][0])
    ind = np.zeros((32, S), np.float32)
    for n in range(32):
        ind[n, n * 256:(n + 1) * 256] = 1.0
    common = {
        "w_in": inp["w_in"][0], "w_pool": np.ascontiguousarray(inp["w_pool"][0].reshape(512, 128)),
        "w_out": inp["w_out"][0], "w_up": inp["w_up"][0], "w_down": inp["w_down"][0],
        "w_proj": inp["w_ple_proj"][0], "w_gate": inp["w_ple_gate"][0],
        "gv": gvv, "ind": ind, "ident": np.eye(128, dtype=np.float32),
    }
    shared = {"common": common, "xT": [np.ascontiguousarray(inp["x"][b].T) for b in range(2)]}
    maps, owns = [], []
    for c in range(NCORES):
        d, own = _core_inputs(c, inp, shared)
        maps.append(d)
        owns.append(own)
    return maps, owns


def kernel(**inputs):
    maps, owns = _prep(inputs)
    nc = build()
    res = run_bass_kernel_spmd(nc, maps, core_ids=list(range(NCORES)))
    out = np.empty((2, S, D), np.float32)
    for c in range(NCORES):
        out[c // 4, owns[c], :] = np.asarray(res.results[c]["outT"], dtype=np.float32).T
    return out
```

```python
import math
from contextlib import ExitStack

import numpy as np
import concourse.bass as bass
import concourse.mybir as mybir
from concourse.bass_utils import run_bass_kernel_spmd

F32 = mybir.dt.float32
BF16 = mybir.dt.bfloat16
AF = mybir.ActivationFunctionType
ALU = mybir.AluOpType

NCORES = 8
S = 8192
D = 1024
NB = 32
NOWN = 2048
EPS = 1e-6
MASKV = 30000.0
ENGS = ("pe", "act", "dve", "pool", "sp")
SB_BASE = 17408
VS = 80


class Op:
    __slots__ = ("eng", "fn", "deps", "dma", "dkey", "dval", "sigval", "idx", "waits")

    def __init__(self, eng, fn, dma, dkey):
        self.eng = eng
        self.fn = fn
        self.deps = set()
        self.dma = dma
        self.dkey = dkey
        self.dval = 0
        self.sigval = None
        self.waits = []


class _Rec:
    def __init__(self):
        self.call = None

    def __getattr__(self, name):
        def f(*a, **k):
            self.call = (name, a, k)
            return None
        return f


class Prog:
    def __init__(self):
        self.ops = []
        self.per_eng = {e: [] for e in ENGS}
        self.last_w = {}
        self.readers = {}
        self.dma_count = {}

    def _add(self, eng, fn, reads, writes, dma, dkey):
        rec = _Rec()
        fn(rec)
        assert rec.call is not None
        op = Op(eng, rec.call, dma, dkey)
        op.idx = len(self.ops)
        for r in reads:
            w = self.last_w.get(r)
            if w is not None:
                op.deps.add(w)
        for k in writes:
            w = self.last_w.get(k)
            if w is not None:
                op.deps.add(w)
            for rd in self.readers.get(k, ()):
                op.deps.add(rd)
        for k in writes:
            self.last_w[k] = op.idx
            self.readers[k] = []
        for r in reads:
            self.readers.setdefault(r, []).append(op.idx)
        op.deps.discard(op.idx)
        if dma:
            c = self.dma_count.get(dkey, 0) + 16
            self.dma_count[dkey] = c
            op.dval = c
        self.ops.append(op)
        self.per_eng[eng].append(op)
        return op

    def op(self, eng, fn, reads=(), writes=()):
        return self._add(eng, fn, tuple(reads), tuple(writes), False, None)

    def dma(self, eng, fn, reads=(), writes=(), key=None):
        writes = tuple(writes)
        if key is None:
            key = writes[0]
        if eng == "pool":
            self.npool = getattr(self, "npool", 0) + 1
            writes = writes + (("poolq", self.npool % 2),)
            key = ("pq", self.npool % 2)
        return self._add(eng, fn, tuple(reads), writes, True, key)

    def barrier(self, new_keys=()):
        keys = list(set(self.last_w) | set(self.readers))
        self._add("sp", lambda e: e.nop(), tuple(keys), tuple(keys) + tuple(new_keys), False, None)

    def finalize(self):
        need = set()
        for op in self.ops:
            for d in op.deps:
                dop = self.ops[d]
                if dop.dma:
                    continue
                if dop.eng == "pe" and op.eng == "pe" and not op.dma:
                    continue
                need.add(d)
        cnt = {e: 0 for e in ENGS}
        for op in self.ops:
            if op.dma:
                continue
            if op.idx in need:
                cnt[op.eng] += 1
                op.sigval = cnt[op.eng]
        seen = {e: {} for e in ENGS}
        for e in ENGS:
            for op in self.per_eng[e]:
                req = {}
                for d in op.deps:
                    dop = self.ops[d]
                    if dop.dma:
                        k = ("d", dop.dkey)
                        v = dop.dval
                    else:
                        if dop.eng == "pe" and op.eng == "pe" and not op.dma:
                            continue
                        k = ("e", dop.eng)
                        v = dop.sigval
                    if v > req.get(k, 0):
                        req[k] = v
                for k, v in req.items():
                    if v > seen[e].get(k, 0):
                        seen[e][k] = v
                        op.waits.append((k, v))
        return cnt

    def emit(self, nc, es, final_eng="sp"):
        cnt = self.finalize()
        esem = {e: es.enter_context(nc.semaphore("sem_" + e)) for e in ENGS}
        dsem = {}
        for i, k in enumerate(self.dma_count):
            dsem[k] = es.enter_context(nc.semaphore("dsem_%d" % i))
        block = es.enter_context(nc.Block())
        handles = {"pe": block.tensor, "act": block.scalar, "dve": block.vector,
                   "pool": block.gpsimd, "sp": block.sync}

        def make(e):
            def body(eng):
                for op in self.per_eng[e]:
                    for (k, v) in op.waits:
                        if k[0] == "d":
                            eng.wait_ge(dsem[k[1]], v)
                        else:
                            eng.wait_ge(esem[k[1]], v)
                    name, a, k = op.fn
                    ins = getattr(eng, name)(*a, **k)
                    if op.dma:
                        ins.then_inc(dsem[op.dkey], 16)
                    elif op.sigval is not None:
                        ins.then_inc(esem[e], 1)
                if e == final_eng:
                    for k, c in self.dma_count.items():
                        eng.wait_ge(dsem[k], c)
                    for e2 in ENGS:
                        if cnt[e2] > 0:
                            eng.wait_ge(esem[e2], cnt[e2])
            return body

        for e in ENGS:
            handles[e](make(e))


class Arena:
    def __init__(self, nc, base, limit):
        self.nc = nc
        self.base = base
        self.off = base
        self.limit = limit
        self.n = 0

    def alloc(self, name, shape, dt):
        per = 1
        for s in shape[1:]:
            per *= s
        nbytes = per * (2 if dt == BF16 else 4)
        nbytes = (nbytes + 63) // 64 * 64
        assert self.off + nbytes <= self.limit, (name, self.off, nbytes, self.limit)
        self.n += 1
        t = self.nc.alloc_sbuf_tensor_at("%s_%d" % (name, self.n), list(shape), dt, offset=self.off)
        self.off += nbytes
        return t

    def mark(self):
        return self.off

    def reset(self, off):
        self.off = off


def build(debug=False):
    nc = bass.Bass("TRN2", target_bir_lowering=False)
    P = Prog()

    def din(name, shape):
        return nc.dram_tensor(name, list(shape), F32, kind="ExternalInput").ap()

    xTa = din("xTa", [D, S])
    xTo = din("xTo", [D, NOWN])
    xTh = din("xTh", [D, 128])
    pTo = din("pTo", [256, NOWN])
    w_in = din("w_in", [D, 2048])
    w_pool = din("w_pool", [512, 128])
    w_out = din("w_out", [D, D])
    w_up = din("w_up", [D, 4096])
    w_down = din("w_down", [4096, D])
    w_proj = din("w_proj", [256, D])
    w_gate = din("w_gate", [D, D])
    gv_d = din("gv", [128, 40])
    b31_d = din("b31", [128, 8])
    BTd = din("BTd", [8, 128, 4096])
    msk_d = din("msk", [128, 3 * 8 * 32])
    invfix_d = din("invfix", [128, 64])
    ind_d = din("ind", [32, S])
    ident_d = din("ident", [128, 128])
    outT = nc.dram_tensor("outT", [D, NOWN], F32, kind="ExternalOutput").ap()

    skind = "ExternalOutput" if debug else "Internal"
    KTd = nc.dram_tensor("KTd", [512, S], BF16, kind=skind).ap()
    Vd2 = nc.dram_tensor("Vd2", [8, 128, 64, VS], BF16, kind=skind).ap()
    kbd = nc.dram_tensor("kbd", [512, 32], F32, kind=skind).ap()
    Yd = nc.dram_tensor("Yd", [D, NOWN], BF16, kind=skind).ap()
    Wup_s = nc.dram_tensor("Wup_s", [8, 128, 8, 512], BF16, kind="Internal").ap()
    Wdn_s = nc.dram_tensor("Wdn_s", [8, 128, 4, 1024], BF16, kind="Internal").ap()
    if debug:
        QAd = nc.dram_tensor("QAd", [8, 96, NOWN], BF16, kind="ExternalOutput").ap()
        dWin = nc.dram_tensor("dWin", [128, 8, 2048], BF16, kind="ExternalOutput").ap()
        dxb = nc.dram_tensor("dxb", [128, 8, 512], BF16, kind="ExternalOutput").ap()
        dsq = nc.dram_tensor("dsq", [128, 8, 512], BF16, kind="ExternalOutput").ap()
        drstd = nc.dram_tensor("drstd", [128, 512], F32, kind="ExternalOutput").ap()
        drcol = nc.dram_tensor("drcol", [128, 4], F32, kind="ExternalOutput").ap()
        dktmp = nc.dram_tensor("dktmp", [128, 512], F32, kind="ExternalOutput").ap()
        dmix = nc.dram_tensor("dmix", [128, 8, 512], F32, kind="ExternalOutput").ap()
        dh1 = nc.dram_tensor("dh1", [128, 8, 512], F32, kind="ExternalOutput").ap()
        dh2 = nc.dram_tensor("dh2", [128, 8, 512], F32, kind="ExternalOutput").ap()
        duT = nc.dram_tensor("duT", [128, 32, 512], BF16, kind="ExternalOutput").ap()
        dfT = nc.dram_tensor("dfT", [128, 8, 512], F32, kind="ExternalOutput").ap()

    with ExitStack() as es:
        AR = Arena(nc, SB_BASE, nc.SBUF_PARTITION_SIZE_BYTES)
        PS = [es.enter_context(nc.psum_tensor("psb%d" % b, [128, 512], F32)) for b in range(8)]

        def pk(b):
            return ("ps", b)

        identf = AR.alloc("identf", [128, 128], F32)
        ones_bf = AR.alloc("ones_bf", [128, 128], BF16)
        onesf = AR.alloc("onesf", [128, 64], F32)
        gv = AR.alloc("gv", [128, 40], F32)
        b31 = AR.alloc("b31", [128, 8], F32)
        m_consts = AR.mark()
        QA = AR.alloc("QA", [128, 8, NOWN], BF16)
        m_after_persist = AR.mark()
        G_MIXPRE, G_MIXPOST, G_MLPPRE, G_MLPPOST, G_POOL = 0, 8, 16, 24, 32

        P.dma("sp", lambda e: e.dma_start(out=identf[:], in_=ident_d[:, :]), writes=["identf"])
        P.dma("sp", lambda e: e.dma_start(out=gv[:], in_=gv_d[:, :]), writes=["gv"])
        P.dma("sp", lambda e: e.dma_start(out=b31[:], in_=b31_d[:, :]), writes=["b31"])
        P.op("dve", lambda e: e.memset(ones_bf[:], 1.0), writes=["ones_bf"])
        P.op("dve", lambda e: e.memset(onesf[:], 1.0), writes=["onesf"])

        w_up_v = w_up.rearrange("(c p) (u f) -> u p c f", p=128, f=512)
        w_dn_v = w_down.rearrange("(v q p) n -> v p q n", q=4, p=128)

        WCAST = [("Wup_s", u) for u in range(8)] + [("Wdn_s", v) for v in range(8)]
        Winb = AR.alloc("Winb", [128, 8, 2048], BF16)
        xf = [AR.alloc("xf", [128, 8, 512], F32) for _ in range(2)]
        xb = [AR.alloc("xb", [128, 8, 512], BF16) for _ in range(2)]
        sq = [AR.alloc("sq", [128, 8, 512], BF16) for _ in range(2)]
        sqt = AR.alloc("sqt", [128, 512], F32)
        rstd = [AR.alloc("rstd", [128, 512], F32) for _ in range(2)]
        sqc = AR.alloc("sqc", [128, 4], F32)
        rcol = [AR.alloc("rcol", [128, 4], F32) for _ in range(2)]
        ktmp = [AR.alloc("ktmp", [128, 512], F32) for _ in range(2)]
        kst = [AR.alloc("kst", [128, 4, 512], BF16) for _ in range(2)]
        m_vst = AR.mark()
        vst = [AR.alloc("vst", [128, 8, 4, VS], BF16) for _ in range(2)]
        kbacc = AR.alloc("kbacc", [128, 4, 32], F32)
        for bfi in range(2):
            P.op("dve", lambda e, bfi=bfi: e.memset(vst[bfi][:], 1.0), writes=[("vst", bfi, sub) for sub in range(4)])
        w_in_v = w_in.rearrange("(c p) n -> p c n", p=128)
        for part in (1, 0):
            for c in range(8):
                P.dma("pool", lambda e, c=c, part=part: e.dma_start(out=Winb[:, c, part * 1024:(part + 1) * 1024],
                                                                     in_=w_in_v[:, c, part * 1024:(part + 1) * 1024]),
                      writes=[("Winb", c, part)])
        WINB = [("Winb", c, part) for c in range(8) for part in range(2)]

        XF = lambda bi: [("xf", bi, 0), ("xf", bi, 1)]

        def issue_load(src_ap, n, bi):
            for hlf in range(2):
                P.dma("sp", lambda e, hlf=hlf: e.dma_start(out=xf[bi][:, 4 * hlf:4 * hlf + 4, 0:n],
                                                            in_=src_ap[:, 4 * hlf:4 * hlf + 4, :]),
                      writes=[("xf", bi, hlf)])

        def pre(n, bi):
            P.op("act", lambda e: e.activation(out=sq[bi][:, :, 0:n], in_=xf[bi][:, :, 0:n], func=AF.Square),
                 reads=XF(bi), writes=[("sq", bi)])
            for c in range(8):
                P.op("dve", lambda e, c=c: e.tensor_scalar(out=xb[bi][:, c, 0:n], in0=xf[bi][:, c, 0:n],
                                                            scalar1=gv[:, G_MIXPRE + c:G_MIXPRE + c + 1], scalar2=None,
                                                            op0=ALU.mult),
                     reads=XF(bi) + ["gv"], writes=[("xb", bi, c)])

        def pre_b(n, bi):
            P.op("act", lambda e: e.activation(out=sq[bi][:, :, 0:n], in_=xf[bi][:, :, 0:n], func=AF.Square),
                 reads=XF(bi), writes=[("sq", bi)])
            for c in range(8):
                P.op("act", lambda e, c=c: e.activation(out=xb[bi][:, c, 0:n], in_=xf[bi][:, c, 0:n], func=AF.Copy,
                                                        scale=gv[:, G_MIXPRE + c:G_MIXPRE + c + 1]),
                     reads=XF(bi) + ["gv"], writes=[("xb", bi, c)])

        def post(n, bi):
            for c in range(8):
                P.op("pe", lambda e, c=c: e.matmul(PS[0][:, 0:n], lhsT=ones_bf[:], rhs=sq[bi][:, c, 0:n],
                                                   start=(c == 0), stop=(c == 7)),
                     reads=[("sq", bi), "ones_bf"], writes=[pk(0)])
            P.op("act", lambda e: e.activation(out=sqt[:, 0:n], in_=PS[0][:, 0:n], func=AF.Ln,
                                               scale=1.0 / D, bias=EPS),
                 reads=[pk(0)], writes=["sqt"])
            P.op("act", lambda e: e.activation(out=rstd[bi][:, 0:n], in_=sqt[:, 0:n], func=AF.Exp, scale=-0.5),
                 reads=["sqt"], writes=[("rstd", bi)])

        def norm(n, bi):
            pre(n, bi)
            post(n, bi)

        xTa_v = xTa.rearrange("(c p) t -> p c t", p=128)
        xTh_v = xTh.rearrange("(c p) t -> p c t", p=128)
        xTo_v = xTo.rearrange("(c p) t -> p c t", p=128)
        issue_load(xTa_v[:, :, 0:512], 512, 0)
        issue_load(xTa_v[:, :, 512:1024], 512, 1)
        pre(512, 0)
        for T in range(16):
            bi = T % 2
            post(512, bi)
            if debug and T == 0:
                P.dma("sp", lambda e: e.dma_start(out=dWin, in_=Winb[:]), reads=WINB, writes=["dWin"])
                P.dma("sp", lambda e: e.dma_start(out=dxb, in_=xb[0][:]), reads=[("xb", 0, c) for c in range(8)], writes=["dxb"])
                P.dma("sp", lambda e: e.dma_start(out=dsq, in_=sq[0][:]), reads=[("sq", 0)], writes=["dsq"])
                P.dma("sp", lambda e: e.dma_start(out=drstd, in_=rstd[0][:]), reads=[("rstd", 0)], writes=["drstd"])
            for sub in range(4):
                for c in range(8):
                    P.op("pe", lambda e, c=c, sub=sub: e.matmul(
                        PS[1][:, sub:sub + 1], lhsT=sq[bi][:, c, sub * 128:(sub + 1) * 128], rhs=ones_bf[:, 0:1],
                        start=(c == 0), stop=(c == 7)), reads=[("sq", bi), "ones_bf"], writes=[pk(1)])
            if T + 1 < 16:
                pre(512, (T + 1) % 2)
            else:
                pre(128, 0)
            if T + 2 < 16:
                issue_load(xTa_v[:, :, (T + 2) * 512:(T + 3) * 512], 512, bi)
            elif T + 2 == 16:
                issue_load(xTh_v, 128, 0)
            P.op("act", lambda e: e.activation(out=sqc[:], in_=PS[1][:, 0:4], func=AF.Ln, scale=1.0 / D, bias=EPS),
                 reads=[pk(1)], writes=["sqc"])
            P.op("act", lambda e, bi=bi: e.activation(out=rcol[bi][:], in_=sqc[:], func=AF.Exp, scale=-0.5),
                 reads=["sqc"], writes=[("rcol", bi)])
            for hp in range(4):
                pb = (2, 3, 6)[(4 * T + hp) % 3]
                for c in range(8):
                    P.op("pe", lambda e, c=c, hp=hp, pb=pb: e.matmul(
                        PS[pb][:], lhsT=Winb[:, c, 1024 + hp * 128:1024 + (hp + 1) * 128], rhs=xb[bi][:, c, :],
                        start=(c == 0), stop=(c == 7)), reads=[("Winb", c, 1), ("xb", bi, c)], writes=[pk(pb)])
                P.op("dve", lambda e, hp=hp, pb=pb: e.tensor_tensor(out=ktmp[hp % 2][:], in0=PS[pb][:], in1=rstd[bi][:],
                                                                     op=ALU.mult),
                     reads=[pk(pb), ("rstd", bi)], writes=[("ktmp", hp % 2)])
                if debug and T == 0 and hp == 0:
                    P.dma("sp", lambda e: e.dma_start(out=dktmp, in_=ktmp[0][:]), reads=[("ktmp", 0)], writes=["dktmp"])
                    P.dma("sp", lambda e: e.dma_start(out=drcol, in_=rcol[0][:]), reads=[("rcol", 0)], writes=["drcol"])
                for blk in range(2):
                    P.op("act", lambda e, hp=hp, blk=blk: e.activation(
                        out=kst[bi][:, hp, blk * 256:(blk + 1) * 256], in_=ktmp[hp % 2][:, blk * 256:(blk + 1) * 256],
                        func=AF.Copy, accum_out=kbacc[:, hp, 2 * T + blk:2 * T + blk + 1]),
                        reads=[("ktmp", hp % 2)], writes=[("kst", bi, hp), ("kbacc", hp)])
            P.dma("sp", lambda e, T=T, bi=bi: e.dma_start(
                out=KTd.rearrange("(hp p) t -> p hp t", p=128)[:, :, T * 512:(T + 1) * 512], in_=kst[bi][:]),
                reads=[("kst", bi, hp) for hp in range(4)], writes=[("KTd", T)], key=("KTd", T % 2))
            for sub in range(4):
                pb = (4, 5, 7)[(4 * T + sub) % 3]
                for c in range(8):
                    P.op("pe", lambda e, c=c, sub=sub, pb=pb: e.matmul(
                        PS[pb][:], lhsT=xb[bi][:, c, sub * 128:(sub + 1) * 128], rhs=Winb[:, c, 1536:2048],
                        start=(c == 0), stop=(c == 7)), reads=[("Winb", c, 1), ("xb", bi, c)], writes=[pk(pb)])
                P.op("act", lambda e, sub=sub, pb=pb: e.activation(
                    out=vst[bi][:, :, sub, 0:64], in_=PS[pb][:].rearrange("p (h d) -> p h d", d=64), func=AF.Copy,
                    scale=rcol[bi][:, sub:sub + 1]), reads=[pk(pb), ("rcol", bi)], writes=[("vst", bi, sub)])
            P.dma("sp", lambda e, T=T, bi=bi: e.dma_start(
                out=Vd2.rearrange("h p kt d -> p h (kt d)")[:, :, 4 * T * VS:(4 * T + 4) * VS],
                in_=vst[bi][:].rearrange("p h s d -> p h (s d)")),
                reads=[("vst", bi, sub) for sub in range(4)], writes=[("Vd2", T)], key=("Vd2", T % 2))
        P.op("dve", lambda e: e.tensor_scalar(out=kbacc[:], in0=kbacc[:], scalar1=1.0 / 256, scalar2=None, op0=ALU.mult),
             reads=[("kbacc", hp) for hp in range(4)], writes=[("kbacc", hp) for hp in range(4)])
        P.dma("sp", lambda e: e.dma_start(out=kbd.rearrange("(hp p) n -> p hp n", p=128), in_=kbacc[:]),
              reads=[("kbacc", hp) for hp in range(4)], writes=["kbd"])

        mB = AR.mark()
        zph = AR.alloc("zph", [128, 4, 128], F32)
        zp0 = AR.alloc("zp", [128, 4, 2, 272], F32)
        m_tmpz = AR.mark()
        AR.reset(m_vst)
        zp1 = AR.alloc("zp", [128, 4, 2, 272], F32)
        assert AR.mark() <= m_vst + 2 * 8 * 4 * VS * 2
        AR.reset(m_tmpz)
        zp = [zp0, zp1]
        VSTK = [("vst", b_, s_) for b_ in range(2) for s_ in range(4)]
        sA = AR.alloc("sA", [128, 2, 272], F32)
        sB = AR.alloc("sB", [128, 2, 272], F32)
        dfix = AR.alloc("dfix", [128, 16], F32)
        dbf = [AR.alloc("dbf", [128, 2, 256], BF16) for _ in range(4)]
        ypst = [AR.alloc("ypst", [128, 512], BF16) for _ in range(2)]
        Wpoolb = AR.alloc("Wpoolb", [128, 4, 128], BF16)
        msk = AR.alloc("msk", [128, 3, 8, 32], F32)
        invfix = AR.alloc("invfix", [128, 4, 16], F32)
        kbTf = AR.alloc("kbTf", [64, 8, 32], F32)
        kbTb = AR.alloc("kbTb", [64, 8, 32], BF16)
        msc2 = [AR.alloc("msc2", [128, 16, 32], F32) for _ in range(2)]
        sel2 = [AR.alloc("sel2", [128, 16, 32], F32) for _ in range(2)]
        Mt2 = [AR.alloc("Mt2", [128, 16, 96], F32) for _ in range(2)]
        top8b = [AR.alloc("top8b", [128, 16, 8], F32) for _ in range(2)]

        P.dma("pool", lambda e: e.dma_start(out=Wpoolb[:], in_=w_pool.rearrange("(g p) n -> p g n", p=128)),
              writes=["Wpoolb"])
        P.dma("sp", lambda e: e.dma_start(out=msk[:], in_=msk_d.rearrange("p (a i n) -> p a i n", a=3, i=8)),
              writes=["msk"])
        P.dma("sp", lambda e: e.dma_start(out=invfix[:], in_=invfix_d.rearrange("p (g r) -> p g r", g=4)),
              writes=["invfix"])
        P.dma("sp", lambda e: e.dma_start(out=kbTf[:], in_=kbd.rearrange("(h d) n -> d h n", d=64)),
              reads=["kbd"], writes=["kbTf"])
        P.op("dve", lambda e: e.tensor_copy(out=kbTb[:], in_=kbTf[:]), reads=["kbTf"], writes=["kbTb"])
        for bk in range(2):
            P.op("dve", lambda e, bk=bk: e.memset(Mt2[bk][:], 0.0), writes=[("Mt2", bk)])

        issue_load(xTo_v[:, :, 0:512], 512, 1)
        post(128, 0)
        for g in range(4):
            pb = 2 + g % 2
            for c in range(8):
                P.op("pe", lambda e, c=c, g=g, pb=pb: e.matmul(
                    PS[pb][:, 0:128], lhsT=Winb[:, c, g * 128:(g + 1) * 128], rhs=xb[0][:, c, 0:128],
                    start=(c == 0), stop=(c == 7)), reads=[("Winb", c, 0), ("xb", 0, c)], writes=[pk(pb)])
            P.op("dve", lambda e, g=g, pb=pb: e.tensor_tensor(out=zph[:, g, :], in0=PS[pb][:, 0:128],
                                                               in1=rstd[0][:, 0:128], op=ALU.mult),
                 reads=[pk(pb), ("rstd", 0)], writes=["zph"])

        def b_norm_q(t, bi):
            post(512, bi)
            for h in range(8):
                pb = 2 + h % 2
                for c in range(8):
                    P.op("pe", lambda e, c=c, h=h, pb=pb: e.matmul(
                        PS[pb][0:64, :], lhsT=Winb[:, c, 512 + h * 64:512 + (h + 1) * 64], rhs=xb[bi][:, c, :],
                        start=(c == 0), stop=(c == 7)), reads=[("Winb", c, 0), ("xb", bi, c)], writes=[pk(pb)])
                P.op("dve", lambda e, h=h, pb=pb: e.scalar_tensor_tensor(
                    out=QA[0:64, h, t * 512:(t + 1) * 512], in0=PS[pb][0:64, :], scalar=0.125, in1=rstd[bi][0:64, :],
                    op0=ALU.mult, op1=ALU.mult), reads=[pk(pb), ("rstd", bi)], writes=[("QAq", h, t)])

        def b_z(t, bi):
            for g in range(4):
                pb = 4 + g % 2
                for c in range(8):
                    P.op("pe", lambda e, c=c, g=g, pb=pb: e.matmul(
                        PS[pb][:], lhsT=Winb[:, c, g * 128:(g + 1) * 128], rhs=xb[bi][:, c, :],
                        start=(c == 0), stop=(c == 7)), reads=[("Winb", c, 0), ("xb", bi, c)], writes=[pk(pb)])
                P.op("dve", lambda e, g=g, pb=pb: e.tensor_tensor(
                    out=zp[t % 2][:, g, :, 16:272], in0=PS[pb][:].rearrange("p (b r) -> p b r", b=2),
                    in1=rstd[bi][:].rearrange("p (b r) -> p b r", b=2), op=ALU.mult),
                    reads=[pk(pb), ("rstd", bi)], writes=[("zp", t % 2, g)] + (VSTK if t % 2 else []))
                P.op("pool", lambda e, g=g: e.tensor_copy(
                    out=zp[t % 2][:, g, :, 0:16], in_=zph[:, g, 32 * t:32 * t + 32].rearrange("p (b r) -> p b r", b=2)),
                    reads=["zph", ("zp", t % 2, g)], writes=[("zp", t % 2, g)])

        def b_mixer(t):
            for g in range(4):
                w = 2 << g
                src = zp[t % 2][:, g]
                bufs = [sA, sB]
                cur = None
                sh = 1
                for k in range(g + 1):
                    dst = bufs[k % 2]
                    lo = 2 * sh - 1
                    a = src if cur is None else cur
                    P.op("dve", lambda e, a=a, dst=dst, lo=lo, sh=sh: e.tensor_tensor(
                        out=dst[:, :, lo:272], in0=a[:, :, lo:272], in1=a[:, :, lo - sh:272 - sh], op=ALU.add),
                        reads=[("zp", t % 2, g), "sA", "sB"], writes=["sA" if k % 2 == 0 else "sB"])
                    cur = dst
                    sh *= 2
                db = dbf[g]
                P.op("dve", lambda e, cur=cur, src=src, db=db, w=w: e.scalar_tensor_tensor(
                    out=db[:], in0=cur[:, :, 16:272], scalar=1.0 / w, in1=src[:, :, 16:272],
                    op0=ALU.mult, op1=ALU.subtract), reads=["sA", "sB", ("zp", t % 2, g)], writes=[("dbf", g)])
                if t == 0:
                    P.op("dve", lambda e, cur=cur, g=g: e.tensor_tensor(out=dfix[:], in0=cur[:, 0, 16:32],
                                                                         in1=invfix[:, g, :], op=ALU.mult),
                         reads=["sA", "sB", "invfix"], writes=["dfix"])
                    P.op("dve", lambda e, src=src, db=db: e.tensor_tensor(out=db[:, 0, 0:16], in0=dfix[:],
                                                                           in1=src[:, 0, 16:32], op=ALU.subtract),
                         reads=["dfix", ("zp", t % 2, g), ("dbf", g)], writes=[("dbf", g)])

        def b_poolout(t):
            for g in range(4):
                db = dbf[g]
                P.op("pe", lambda e, g=g, db=db: e.matmul(PS[6][:], lhsT=Wpoolb[:, g, :],
                                                          rhs=db[:].rearrange("p b r -> p (b r)"),
                                                          start=True, stop=True),
                     reads=["Wpoolb", ("dbf", g)], writes=[pk(6)])
                P.op("act", lambda e, g=g: e.activation(out=ypst[g % 2][:], in_=PS[6][:], func=AF.Copy,
                                                        scale=gv[:, G_POOL + g:G_POOL + g + 1]),
                     reads=[pk(6), "gv"], writes=[("ypst", g % 2)])
                P.dma("sp", lambda e, g=g: e.dma_start(out=Yd[g * 128:(g + 1) * 128, t * 512:(t + 1) * 512],
                                                        in_=ypst[g % 2][:]),
                      reads=[("ypst", g % 2)], writes=[("Yd", g, t)], key=("Ydp", g % 2))

        def b_scores(t):
            for qq in range(4):
                qt = 4 * t + qq
                bank = 7 if qq < 2 else 1
                for h in range(8):
                    col = (qq % 2) * 256 + h * 32
                    P.op("pe", lambda e, h=h, qt=qt, bank=bank, col=col: e.matmul(
                        PS[bank][:, col:col + 32], lhsT=QA[0:64, h, qt * 128:(qt + 1) * 128], rhs=kbTb[:, h, :],
                        start=True, stop=True), reads=[("QAq", h, t), "kbTb"], writes=[pk(bank)])

        def b_sel(t):
            for bk in range(2):
                bank = 7 if bk == 0 else 1
                iblk = 2 * t + bk
                bshape = [128, 16, 32]
                P.op("dve", lambda e, bk=bk, bank=bank, iblk=iblk: e.tensor_tensor(
                    out=msc2[bk][:], in0=PS[bank][:].rearrange("p (a n) -> p a n", n=32),
                    in1=msk[:, 0, iblk:iblk + 1, :].broadcast_to(bshape), op=ALU.add),
                    reads=[pk(bank), "msk"], writes=[("msc2", bk)])
                for a in range(16):
                    P.op("dve", lambda e, bk=bk, a=a: e.max(out=top8b[bk][:, a, :], in_=msc2[bk][:, a, :]),
                         reads=[("msc2", bk)], writes=[("top8b", bk, a)])
                P.op("dve", lambda e, bk=bk: e.tensor_tensor(out=sel2[bk][:], in0=msc2[bk][:],
                                                             in1=top8b[bk][:, :, 2:3].broadcast_to(bshape), op=ALU.is_ge),
                     reads=[("msc2", bk)] + [("top8b", bk, a) for a in range(16)], writes=[("sel2", bk)])
                P.op("dve", lambda e, bk=bk, iblk=iblk: e.tensor_tensor(out=sel2[bk][:], in0=sel2[bk][:],
                                                                         in1=msk[:, 1, iblk:iblk + 1, :].broadcast_to(bshape),
                                                                         op=ALU.mult),
                     reads=[("sel2", bk), "msk"], writes=[("sel2", bk)])
                P.op("dve", lambda e, bk=bk, iblk=iblk: e.tensor_tensor(out=sel2[bk][:], in0=sel2[bk][:],
                                                                         in1=msk[:, 2, iblk:iblk + 1, :].broadcast_to(bshape),
                                                                         op=ALU.max),
                     reads=[("sel2", bk), "msk"], writes=[("sel2", bk)])
                P.op("dve", lambda e, bk=bk: e.tensor_scalar(out=Mt2[bk][:, :, 64:96], in0=sel2[bk][:], scalar1=MASKV,
                                                             scalar2=-MASKV, op0=ALU.mult, op1=ALU.add),
                     reads=[("sel2", bk)], writes=[("Mt2", bk)])

        def b_transposes(t):
            for bk in range(2):
                for qtl in range(2):
                    qt = 4 * t + 2 * bk + qtl
                    for hh in range(2):
                        pb = 2 + 2 * qtl + hh
                        for h4 in range(4):
                            h = hh * 4 + h4
                            P.op("pe", lambda e, h=h, h4=h4, pb=pb, bk=bk, qtl=qtl: e.transpose(
                                PS[pb][0:96, h4 * 128:(h4 + 1) * 128], Mt2[bk][:, qtl * 8 + h, :], identf[:]),
                                reads=[("Mt2", bk), "identf"], writes=[pk(pb)])
                        P.op("act", lambda e, hh=hh, pb=pb, qt=qt: e.activation(
                            out=QA[64:96, hh * 4:hh * 4 + 4, qt * 128:(qt + 1) * 128],
                            in_=PS[pb][64:96, :].rearrange("p (h q) -> p h q", h=4), func=AF.Copy),
                            reads=[pk(pb)], writes=[("QAm", hh, t)])

        pre_b(512, 1)
        issue_load(xTo_v[:, :, 512:1024], 512, 0)
        for t in range(4):
            bi = (t + 1) % 2
            b_norm_q(t, bi)
            b_scores(t)
            b_z(t, bi)
            if t > 0:
                b_mixer(t - 1)
            if t + 1 < 4:
                pre_b(512, t % 2)
            if t + 2 < 4:
                issue_load(xTo_v[:, :, (t + 2) * 512:(t + 3) * 512], 512, bi)
            if t > 0:
                b_transposes(t - 1)
                b_poolout(t - 1)
            b_sel(t)
        b_mixer(3)
        b_transposes(3)
        b_poolout(3)
        if debug:
            for h in range(8):
                P.dma("sp", lambda e, h=h: e.dma_start(out=QAd[h], in_=QA[0:96, h, :]),
                      reads=[("QAq", h, t) for t in range(4)] + [("QAm", h // 4, t) for t in range(4)],
                      writes=[("QAd", h)], key="QAd")

        AR.reset(m_after_persist)
        KA = [AR.alloc("KA", [128, S], BF16) for _ in range(2)]
        VA = [AR.alloc("VA", [128, 64 * VS + 64], BF16) for _ in range(2)]
        BT = [AR.alloc("BT", [128, 8, 512], F32) for _ in range(2)]
        PT = [AR.alloc("PT", [128, 512], BF16) for _ in range(6)]
        tS = [AR.alloc("tS", [128, 512], F32) for _ in range(4)]
        Osb = [AR.alloc("Osb", [128, 256], F32) for _ in range(2)]
        rrow = [AR.alloc("rrow", [128, 256], F32) for _ in range(2)]
        yst = [AR.alloc("yst", [64, 256], BF16) for _ in range(2)]
        m_wstart = AR.mark()
        Woutb = AR.alloc("Woutb", [128, 8, 1024], BF16)
        Wgateb = AR.alloc("Wgateb", [128, 8, 1024], BF16)
        Wprojb = AR.alloc("Wprojb", [128, 2, 1024], BF16)
        Wupb = [AR.alloc("Wupb", [128, 8, 512], BF16) for _ in range(2)]
        Wdnb = [AR.alloc("Wdnb", [128, 4, 1024], BF16) for _ in range(2)]
        newk = ["Woutb", "Wgateb", "Wprojb", ("Wupb", 0), ("Wupb", 1), ("Wdnb", 0), ("Wdnb", 1)] + \
               [("KA", i, hl) for i in range(2) for hl in range(2)] + [("KAi", i) for i in range(2)] + \
               [("VA", i) for i in range(2)] + [("VApad", i) for i in range(2)] + [("BT", i, hl) for i in range(2) for hl in range(2)] + [("PT", i) for i in range(6)] + [("tS", i) for i in range(4)] + \
               [("Osb", i) for i in range(2)] + [("rrow", i) for i in range(2)] + [("yst", i) for i in range(2)]
        P.barrier(newk)
        for bfi in range(2):
            P.op("dve", lambda e, bfi=bfi: e.memset(VA[bfi][:, 64 * VS:64 * VS + 64], 0.0), writes=[("VApad", bfi)])
            P.dma("pool", lambda e, bfi=bfi: e.dma_start(out=KA[bfi][64:96, :].rearrange("n (a b) -> n a b", b=1024),
                                                         in_=ind_d.rearrange("n (a b) -> n a b", b=1024)), writes=[("KAi", bfi)])

        for u in range(8):
            P.dma("pool", lambda e, u=u: e.dma_start(out=Wup_s[u], in_=w_up_v[u]), writes=[("Wup_s", u)])
        for v in range(8):
            P.dma("pool", lambda e, v=v: e.dma_start(out=Wdn_s[v], in_=w_dn_v[v]), writes=[("Wdn_s", v)])

        P.dma("pool", lambda e: e.dma_start(out=Woutb[:], in_=w_out.rearrange("(c p) n -> p c n", p=128)), writes=["Woutb"])
        P.dma("pool", lambda e: e.dma_start(out=Wgateb[:], in_=w_gate.rearrange("(c p) n -> p c n", p=128)), writes=["Wgateb"])
        P.dma("pool", lambda e: e.dma_start(out=Wprojb[:], in_=w_proj.rearrange("(c p) n -> p c n", p=128)), writes=["Wprojb"])

        def load_head(h):
            bfi = h % 2
            for hlf in range(2):
                P.dma("sp", lambda e, hlf=hlf: e.dma_start(out=KA[bfi][0:64, hlf * 4096:(hlf + 1) * 4096],
                                                            in_=KTd[h * 64:(h + 1) * 64, hlf * 4096:(hlf + 1) * 4096]),
                      reads=[("KTd", T) for T in range(16)], writes=[("KA", bfi, hlf)])
            P.dma("sp", lambda e: e.dma_start(out=VA[bfi][:, 0:64 * VS], in_=Vd2[h].rearrange("p k d -> p (k d)")),
                  reads=[("Vd2", T) for T in range(16)], writes=[("VA", bfi)])
            for hlf in range(2):
                P.dma("sp", lambda e, hlf=hlf: e.dma_start(out=BT[bfi][:, 4 * hlf:4 * hlf + 4, :],
                                                            in_=BTd[h, :, 2048 * hlf:2048 * (hlf + 1)].rearrange(
                                                                "p (s x) -> p s x", s=4)),
                      writes=[("BT", bfi, hlf)])

        def shift_bias(h):
            bfi = h % 2
            for hlf in range(2):
                P.op("dve", lambda e, hlf=hlf: e.tensor_scalar(out=BT[bfi][:, 4 * hlf:4 * hlf + 4, :],
                                                                in0=BT[bfi][:, 4 * hlf:4 * hlf + 4, :],
                                                                scalar1=b31[:, h:h + 1], scalar2=None, op0=ALU.subtract),
                     reads=[("BT", bfi, hlf), "b31"], writes=[("BT", bfi, hlf)])

        pairs = []
        for h in range(8):
            for i in range(8):
                ns = list(range(0, 4 * i + 4))
                for idx, n in enumerate(ns):
                    pairs.append((h, i, n, idx == 0, idx == len(ns) - 1))

        def emit_S(pi):
            h, i, n, first, last = pairs[pi]
            bfi = h % 2
            sb = SBANK[pi % 5]
            for hf in range(2):
                P.op("pe", lambda e, hf=hf: e.matmul(PS[sb][:, hf * 256:(hf + 1) * 256],
                                                     lhsT=KA[bfi][0:96, n * 256 + hf * 128:n * 256 + (hf + 1) * 128],
                                                     rhs=QA[0:96, h, i * 256:(i + 1) * 256], start=True, stop=True),
                     reads=[("KA", bfi, 0), ("KA", bfi, 1), ("KAi", bfi), ("QAq", h, i // 2), ("QAm", h // 4, i // 2)], writes=[pk(sb)])
            near = n >= 4 * i - 4
            pt = pi % 6
            if near:
                s = n - (4 * i - 4)
                ts = pi % 4
                P.op("dve", lambda e: e.tensor_tensor(out=tS[ts][:], in0=PS[sb][:], in1=BT[bfi][:, s, :], op=ALU.add),
                     reads=[pk(sb), ("BT", bfi, 0), ("BT", bfi, 1)], writes=[("tS", ts)])
                P.op("act", lambda e: e.activation(out=PT[pt][:], in_=tS[ts][:], func=AF.Exp),
                     reads=[("tS", ts)], writes=[("PT", pt)])
            else:
                P.op("act", lambda e: e.activation(out=PT[pt][:], in_=PS[sb][:], func=AF.Exp),
                     reads=[pk(sb)], writes=[("PT", pt)])

        def emit_PV(pi, extra=()):
            h, i, n, first, last = pairs[pi]
            bfi = h % 2
            pt = pi % 6
            ob = 3 + i % 2
            for hf in range(2):
                P.op("pe", lambda e, hf=hf: e.matmul(PS[ob][:, 0:256],
                                                     lhsT=VA[bfi][:, (2 * n + hf) * VS:(2 * n + hf) * VS + 128],
                                                     rhs=PT[pt][:, hf * 256:(hf + 1) * 256],
                                                     start=(first and hf == 0), stop=(last and hf == 1)),
                     reads=[("VA", bfi), ("VApad", bfi), ("PT", pt)] + list(extra), writes=[pk(ob)])

        def emit_epi1(pi):
            h, i, n, first, last = pairs[pi]
            o = i % 2
            ob = 3 + i % 2
            P.op("act", lambda e: e.activation(out=Osb[o][0:65, :], in_=PS[ob][0:65, 0:256], func=AF.Copy),
                 reads=[pk(ob)], writes=[("Osb", o)])
            P.op("dve", lambda e: e.reciprocal(out=rrow[o][64:65, :], in_=Osb[o][64:65, :]),
                 reads=[("Osb", o)], writes=[("rrow", o)])

        def emit_epi(pi):
            h, i, n, first, last = pairs[pi]
            o = i % 2
            P.op("pe", lambda e: e.matmul(PS[5][0:64, 0:256], lhsT=onesf[64:65, 0:64], rhs=rrow[o][64:65, :],
                                          start=True, stop=True), reads=[("rrow", o), "onesf"], writes=[pk(5)])
            P.op("dve", lambda e: e.tensor_tensor(out=yst[o][:], in0=Osb[o][0:64, :], in1=PS[5][0:64, 0:256],
                                                  op=ALU.mult), reads=[("Osb", o), pk(5)], writes=[("yst", o)])
            P.dma("sp", lambda e: e.dma_start(out=Yd[512 + h * 64:512 + (h + 1) * 64, i * 256:(i + 1) * 256],
                                              in_=yst[o][:]),
                  reads=[("yst", o)], writes=[("Yd", "a", h, i)], key=("Yda", o))

        SBANK = [0, 1, 2, 6, 7]
        load_head(0)
        shift_bias(0)
        LOOK = 4
        EPI1 = 4
        EPI2 = 10
        npairs = len(pairs)

        def after_pv(pv):
            h, i, n, first, last = pairs[pv]
            if first and i == 0 and h + 1 < 8:
                load_head(h + 1)
            if first and i == 4 and h + 1 < 8:
                shift_bias(h + 1)

        for pi in range(0, npairs + LOOK + EPI2 + 2, 2):
            if pi < npairs:
                emit_S(pi)
            pv0, pv1 = pi - LOOK, pi - LOOK + 1
            if 0 <= pv0 < npairs:
                extra = [("PT", pv1 % 6)] if pv1 < npairs else []
                emit_PV(pv0, extra)
                after_pv(pv0)
            if 0 <= pv1 < npairs:
                emit_PV(pv1)
                after_pv(pv1)
            if pi + 1 < npairs:
                emit_S(pi + 1)
            for stp in (pi, pi + 1):
                e1 = stp - LOOK - EPI1
                if 0 <= e1 < npairs and pairs[e1][4]:
                    emit_epi1(e1)
                e2 = stp - LOOK - EPI2
                if 0 <= e2 < npairs and pairs[e2][4]:
                    emit_epi(e2)

        AR.reset(m_consts)
        mixin = AR.alloc("mixin", [128, 8, 512], BF16)
        xo = AR.alloc("xo", [128, 8, 512], F32)
        pf = AR.alloc("pf", [128, 2, 512], F32)
        pbb = AR.alloc("pbb", [128, 2, 512], BF16)
        hT = AR.alloc("hT", [128, 8, 512], F32)
        sq2 = AR.alloc("sq2", [128, 8, 512], BF16)
        hb = AR.alloc("hb", [128, 8, 512], BF16)
        m_uT = AR.mark()
        uT = AR.alloc("uT", [128, 32, 512], BF16)
        m_tmp = AR.mark()
        AR.reset(m_uT)
        mixT = AR.alloc("mixT", [128, 8, 512], F32)
        AR.reset(m_tmp)
        fT = AR.alloc("fT", [128, 8, 512], F32)
        tt = [AR.alloc("tt", [128, 512], F32) for _ in range(2)]
        sqt2 = AR.alloc("sqt2", [128, 512], F32)
        rs2 = AR.alloc("rs2", [128, 512], F32)
        sg = [AR.alloc("sg", [128, 512], F32) for _ in range(2)]
        assert AR.mark() <= m_wstart, (AR.mark(), m_wstart)
        newk = ["mixin", ("xo", 0), ("xo", 1), "pf",
                "pbb", "sqt2", "rs2", ("tt", 0), ("tt", 1), ("sg", 0), ("sg", 1)] + \
               [("hT", m) for m in range(8)] + [("uT", f) for f in range(32)] + [("fT", m) for m in range(8)] + \
               [("sq2", m) for m in range(8)] + [("hb", m) for m in range(8)]
        P.barrier(newk)

        def stats(src_keys):
            for m in range(8):
                P.op("pe", lambda e, m=m: e.matmul(PS[0][:], lhsT=ones_bf[:], rhs=sq2[:, m, :], start=(m == 0), stop=(m == 7)),
                     reads=[("sq2", m), "ones_bf"], writes=[pk(0)])
            P.op("act", lambda e: e.activation(out=sqt2[:], in_=PS[0][:], func=AF.Ln, scale=1.0 / D, bias=EPS),
                 reads=[pk(0)], writes=["sqt2"])
            P.op("act", lambda e: e.activation(out=rs2[:], in_=sqt2[:], func=AF.Exp, scale=-0.5),
                 reads=["sqt2"], writes=["rs2"])

        Yd_v = Yd.rearrange("(c p) t -> p c t", p=128)
        pTo_v = pTo.rearrange("(c p) t -> p c t", p=128)
        outT_v = outT.rearrange("(c p) t -> p c t", p=128)
        yd_keys = [("Yd", g, t) for g in range(4) for t in range(4)] + [("Yd", "a", h, i) for h in range(8) for i in range(8)]
        def d_loads(t):
            tsl = slice(t * 512, (t + 1) * 512)
            P.dma("sp", lambda e: e.dma_start(out=mixin[:], in_=Yd_v[:, :, tsl]), reads=yd_keys, writes=["mixin"])
            for hlf in range(2):
                P.dma("sp", lambda e, hlf=hlf: e.dma_start(out=xo[:, 4 * hlf:4 * hlf + 4, :],
                                                            in_=xTo_v[:, 4 * hlf:4 * hlf + 4, tsl]), writes=[("xo", hlf)])
            P.dma("sp", lambda e: e.dma_start(out=pf[:], in_=pTo_v[:, :, tsl]), writes=["pf"])

        d_loads(0)
        for t in range(4):
            tsl = slice(t * 512, (t + 1) * 512)
            P.op("pool", lambda e: e.tensor_copy(out=pbb[:], in_=pf[:]), reads=["pf"], writes=["pbb"])
            for m in range(8):
                pb = 1 + m % 3
                for c in range(8):
                    P.op("pe", lambda e, c=c, m=m, pb=pb: e.matmul(PS[pb][:], lhsT=Woutb[:, c, m * 128:(m + 1) * 128],
                                                                    rhs=mixin[:, c, :], start=(c == 0), stop=(c == 7)),
                         reads=["Woutb", "mixin"], writes=[pk(pb)])
                P.op("act", lambda e, m=m, pb=pb: e.activation(out=mixT[:, m, :], in_=PS[pb][:], func=AF.Copy),
                     reads=[pk(pb)], writes=[("uT", 2 * m), ("uT", 2 * m + 1)])
                P.op("dve", lambda e, m=m: e.tensor_tensor(out=sq2[:, m, :], in0=mixT[:, m, :], in1=mixT[:, m, :], op=ALU.mult),
                     reads=[("uT", 2 * m), ("uT", 2 * m + 1)], writes=[("sq2", m)])
            if debug and t == 0:
                P.dma("sp", lambda e: e.dma_start(out=dmix, in_=mixT[:]), reads=[("uT", f) for f in range(16)], writes=["dmix"])
            stats(None)
            for m in range(8):
                P.op("dve", lambda e, m=m: e.scalar_tensor_tensor(out=hT[:, m, :], in0=mixT[:, m, :],
                                                                   scalar=gv[:, G_MIXPOST + m:G_MIXPOST + m + 1],
                                                                   in1=rs2[:], op0=ALU.mult, op1=ALU.mult),
                     reads=[("uT", 2 * m), ("uT", 2 * m + 1), "rs2", "gv"], writes=[("hT", m)])
                P.op("dve", lambda e, m=m: e.tensor_tensor(out=hT[:, m, :], in0=hT[:, m, :], in1=xo[:, m, :], op=ALU.add),
                     reads=[("hT", m), ("xo", m // 4)], writes=[("hT", m)])
                P.op("act", lambda e, m=m: e.activation(out=hb[:, m, :], in_=hT[:, m, :], func=AF.Copy,
                                                        scale=gv[:, G_MLPPRE + m:G_MLPPRE + m + 1]),
                     reads=[("hT", m), "gv"], writes=[("hb", m)])
                P.op("act", lambda e, m=m: e.activation(out=sq2[:, m, :], in_=hT[:, m, :], func=AF.Square),
                     reads=[("hT", m)], writes=[("sq2", m)])
            if debug and t == 0:
                P.dma("sp", lambda e: e.dma_start(out=dh1, in_=hT[:]), reads=[("hT", m) for m in range(8)], writes=["dh1"])
            stats(None)
            for u in range(8):
                P.dma("sp", lambda e, u=u: e.dma_start(out=Wupb[u % 2][:], in_=Wup_s[u]),
                      reads=WCAST, writes=[("Wupb", u % 2)])
                for fq in range(4):
                    fc = 4 * u + fq
                    pb = 1 + fc % 3
                    for c in range(8):
                        P.op("pe", lambda e, c=c, fq=fq, pb=pb, u=u: e.matmul(
                            PS[pb][:], lhsT=Wupb[u % 2][:, c, fq * 128:(fq + 1) * 128], rhs=hb[:, c, :],
                            start=(c == 0), stop=(c == 7)), reads=[("Wupb", u % 2), ("hb", c)], writes=[pk(pb)])
                    P.op("dve", lambda e, fc=fc, pb=pb: e.scalar_tensor_tensor(out=tt[fc % 2][:], in0=PS[pb][:], scalar=0.0,
                                                                                in1=rs2[:], op0=ALU.max, op1=ALU.mult),
                         reads=[pk(pb), "rs2"], writes=[("tt", fc % 2)])
                    P.op("act", lambda e, fc=fc: e.activation(out=uT[:, fc, :], in_=tt[fc % 2][:], func=AF.Square),
                         reads=[("tt", fc % 2)], writes=[("uT", fc)])
            if debug and t == 0:
                P.dma("sp", lambda e: e.dma_start(out=duT, in_=uT[:]), reads=[("uT", f) for f in range(32)], writes=["duT"])
            for v in range(8):
                P.dma("sp", lambda e, v=v: e.dma_start(out=Wdnb[v % 2][:], in_=Wdn_s[v]),
                      reads=WCAST, writes=[("Wdnb", v % 2)])
                for fq in range(4):
                    fc = 4 * v + fq
                    for m in range(8):
                        P.op("pe", lambda e, fq=fq, fc=fc, m=m, v=v: e.matmul(
                            PS[m][:], lhsT=Wdnb[v % 2][:, fq, m * 128:(m + 1) * 128], rhs=uT[:, fc, :],
                            start=(fc == 0), stop=(fc == 31)), reads=[("Wdnb", v % 2), ("uT", fc)], writes=[pk(m)])
            for m in range(8):
                P.op("act", lambda e, m=m: e.activation(out=fT[:, m, :], in_=PS[m][:], func=AF.Copy),
                     reads=[pk(m)], writes=[("fT", m)])
                P.op("dve", lambda e, m=m: e.tensor_tensor(out=sq2[:, m, :], in0=fT[:, m, :], in1=fT[:, m, :], op=ALU.mult),
                     reads=[("fT", m)], writes=[("sq2", m)])
            stats(None)
            for m in range(8):
                P.op("dve", lambda e, m=m: e.scalar_tensor_tensor(out=fT[:, m, :], in0=fT[:, m, :],
                                                                   scalar=gv[:, G_MLPPOST + m:G_MLPPOST + m + 1],
                                                                   in1=rs2[:], op0=ALU.mult, op1=ALU.mult),
                     reads=[("fT", m), "rs2", "gv"], writes=[("fT", m)])
                P.op("dve", lambda e, m=m: e.tensor_tensor(out=hT[:, m, :], in0=hT[:, m, :], in1=fT[:, m, :], op=ALU.add),
                     reads=[("hT", m), ("fT", m)], writes=[("hT", m)])
                P.op("act", lambda e, m=m: e.activation(out=hb[:, m, :], in_=hT[:, m, :], func=AF.Copy),
                     reads=[("hT", m)], writes=[("hb", m)])
            if debug and t == 0:
                P.dma("sp", lambda e: e.dma_start(out=dh2, in_=hT[:]), reads=[("hT", m) for m in range(8)], writes=["dh2"])
                P.dma("sp", lambda e: e.dma_start(out=dfT, in_=fT[:]), reads=[("fT", m) for m in range(8)], writes=["dfT"])
            gate_first = True
            for m in range(8):
                pg = 1 + (2 * m) % 6
                pp = 1 + (2 * m + 1) % 6
                for c in range(8):
                    P.op("pe", lambda e, c=c, m=m, pg=pg: e.matmul(PS[pg][:], lhsT=Wgateb[:, c, m * 128:(m + 1) * 128],
                                                                    rhs=hb[:, c, :], start=(c == 0), stop=(c == 7)),
                         reads=["Wgateb", ("hb", c)], writes=[pk(pg)])
                for c in range(2):
                    P.op("pe", lambda e, c=c, m=m, pp=pp: e.matmul(PS[pp][:], lhsT=Wprojb[:, c, m * 128:(m + 1) * 128],
                                                                    rhs=pbb[:, c, :], start=(c == 0), stop=(c == 1)),
                         reads=["Wprojb", "pbb"], writes=[pk(pp)])
                P.op("act", lambda e, m=m, pg=pg: e.activation(out=sg[m % 2][:], in_=PS[pg][:], func=AF.Sigmoid),
                     reads=[pk(pg)], writes=[("sg", m % 2)])
                P.op("dve", lambda e, m=m, pp=pp: e.tensor_tensor(out=sg[m % 2][:], in0=sg[m % 2][:], in1=PS[pp][:],
                                                                   op=ALU.mult),
                     reads=[("sg", m % 2), pk(pp)], writes=[("sg", m % 2)])
                P.op("dve", lambda e, m=m: e.tensor_tensor(out=fT[:, m, :], in0=sg[m % 2][:], in1=hT[:, m, :], op=ALU.add),
                     reads=[("sg", m % 2), ("hT", m), ("fT", m)], writes=[("fT", m)])
            if t + 1 < 4:
                d_loads(t + 1)
            if t == 3:
                for q4 in range(4):
                    P.dma("sp", lambda e, tsl=tsl, q4=q4: e.dma_start(out=outT_v[:, 2 * q4:2 * q4 + 2, tsl],
                                                                       in_=fT[:, 2 * q4:2 * q4 + 2, :]),
                          reads=[("fT", m) for m in range(2 * q4, 2 * q4 + 2)], writes=[("out", t, q4)],
                          key=("out", q4 % 2))
                continue
            for hlf in range(2):
                P.dma("sp", lambda e, tsl=tsl, hlf=hlf: e.dma_start(out=outT_v[:, 4 * hlf:4 * hlf + 4, tsl],
                                                                     in_=fT[:, 4 * hlf:4 * hlf + 4, :]),
                      reads=[("fT", m) for m in range(4 * hlf, 4 * hlf + 4)], writes=[("out", t, hlf)], key=("out", hlf))

        P.emit(nc, es)
    return nc


def _t5_bucket(d):
    n = np.maximum(d, 0)
    nf = np.maximum(n, 1).astype(np.float32)
    large = 16 + (np.log(nf / np.float32(16)) / np.float32(math.log(1024 / 16)) * np.float32(16)).astype(np.int32)
    large = np.minimum(large, 31)
    return np.where(n < 16, n, large)


def _core_inputs(c, inp, shared):
    b, j = c // 4, c % 4
    x = inp["x"][b]
    own = np.concatenate([np.arange((4 * i + j) * 256, (4 * i + j + 1) * 256) for i in range(8)])
    xT = shared["xT"][b]
    xTo = np.ascontiguousarray(xT[:, own])
    xTh = np.zeros((D, 128), np.float32)
    for i in range(8):
        s0 = (4 * i + j) * 256
        if s0 >= 16:
            xTh[:, i * 16:(i + 1) * 16] = xT[:, s0 - 16:s0]
    pTo = np.ascontiguousarray(inp["p"][0, b][own].T)
    rel_bias = inp["rel_bias"]
    k = np.arange(128)[:, None, None, None]
    s = np.arange(8)[None, :, None, None]
    hf = np.arange(2)[None, None, :, None]
    q = np.arange(256)[None, None, None, :]
    delta = 4 + j - s
    dist = delta * 256 + q - (hf * 128 + k)
    bucket = _t5_bucket(dist)
    BTt = np.empty((8, 128, 8, 2, 256), np.float32)
    for h in range(8):
        tb = rel_bias[:, h][bucket]
        tb = np.where(dist < 0, np.float32(-MASKV), tb)
        tb = np.where(np.broadcast_to(delta, tb.shape) < 0, np.float32(0.0), tb)
        BTt[h] = tb
    msk = np.zeros((3, 8, 32), np.float32)
    for i in range(8):
        g = 4 * i + j
        n = np.arange(32)
        msk[0, i] = np.where(n < g, 0.0, -1e30)
        msk[1, i] = (n < g)
        msk[2, i] = (n == g)
    invfix = np.zeros((4, 16), np.float32)
    for g in range(4):
        w = 2 << g
        if j == 0:
            invfix[g] = 1.0 / np.minimum(np.arange(16) + 1, w)
        else:
            invfix[g] = 1.0 / w
    d = dict(shared["common"])
    d.update({
        "xTa": xT, "xTo": xTo, "xTh": xTh, "pTo": pTo,
        "BTd": BTt.reshape(8, 128, 4096),
        "msk": np.ascontiguousarray(np.broadcast_to(msk.reshape(1, -1), (128, 768))),
        "invfix": np.ascontiguousarray(np.broadcast_to(invfix.reshape(1, -1), (128, 64))),
        "b31": np.ascontiguousarray(np.broadcast_to(rel_bias[31][None, :], (128, 8))),
    })
    return d, own


def _prep(inp):
    inp = {k: np.asarray(v, dtype=np.float32) for k, v in inp.items()}
    cols = lambda g: np.ascontiguousarray(g.reshape(-1, 128).T)
    gvv = np.zeros((128, 40), np.float32)
    gvv[:, 0:8] = cols(inp["g_mix_pre"][0])
    gvv[:, 8:16] = cols(inp["g_mix_post"][0])
    gvv[:, 16:24] = cols(inp["g_mlp_pre"][0])
    gvv[:, 24:32] = cols(inp["g_mlp_post"][0])
    gvv[:, 32:36] = cols(inp["pool_scale"][0])
    ind = np.zeros((32, S), np.float32)
    for n in range(32):
        ind[n, n * 256:(n + 1) * 256] = 1.0
    common = {
        "w_in": inp["w_in"][0], "w_pool": np.ascontiguousarray(inp["w_pool"][0].reshape(512, 128)),
        "w_out": inp["w_out"][0], "w_up": inp["w_up"][0], "w_down": inp["w_down"][0],
        "w_proj": inp["w_ple_proj"][0], "w_gate": inp["w_ple_gate"][0],
        "gv": gvv, "ind": ind, "ident": np.eye(128, dtype=np.float32),
    }
    shared = {"common": common, "xT": [np.ascontiguousarray(inp["x"][b].T) for b in range(2)]}
    maps, owns = [], []
    for c in range(NCORES):
        d, own = _core_inputs(c, inp, shared)
        maps.append(d)
        owns.append(own)
    return maps, owns


def kernel(**inputs):
    maps, owns = _prep(inputs)
    nc = build()
    res = run_bass_kernel_spmd(nc, maps, core_ids=list(range(NCORES)))
    out = np.empty((2, S, D), np.float32)
    for c in range(NCORES):
        out[c // 4, owns[c], :] = np.asarray(res.results[c]["outT"], dtype=np.float32).T
    return out
```
